# Optimizing a Trainium2 kernel written in Bass

```python
import math
import jax, jax.numpy as jnp
from jax import lax
import numpy as np

D_MODEL = 1024
BATCH = 8
SEQ = 2048
DEPTH = 2

N_HEADS_A = 4
HEAD_DIM_A = 128
WIDTH_A = N_HEADS_A * HEAD_DIM_A
CONV_A = 4
CHUNK = 64
SSM_WIDTH = 512
SSM_GROUP = 16
N_GROUPS = SSM_WIDTH // SSM_GROUP
SSM_STATE = 64
D_IN_AB = 4 * WIDTH_A + 2 * N_HEADS_A + SSM_WIDTH
D_MIX_AB = WIDTH_A + SSM_WIDTH
POOL_WINDOWS = (2, 4, 8, 16)
POOL_GROUP = D_MODEL // len(POOL_WINDOWS)
MEM_LEN = 256
N_HEADS_X = 4
HEAD_DIM_X = D_MODEL // N_HEADS_X
D_FF = 2816
CONV_FFN = 3
RMS_EPS = 1e-6
N_EVEN = (DEPTH + 1) // 2
N_ODD = DEPTH // 2

kernel_name = "hybrid_deltanet_s5_pool_decoder"


def rms_norm(x, g):
    xf = x.astype(jnp.float32)
    y = xf * lax.rsqrt(jnp.mean(xf * xf, axis=-1, keepdims=True) + RMS_EPS)
    return (y * g.astype(jnp.float32)).astype(x.dtype)


def causal_dwconv(x, w):
    k_w = w.shape[0]
    t = x.shape[1]
    xp = jnp.pad(x, ((0, 0), (k_w - 1, 0), (0, 0)))
    return sum(xp[:, i:i + t, :] * w[i] for i in range(k_w))


def l2_normalize(x):
    return x * lax.rsqrt(jnp.sum(x * x, axis=-1, keepdims=True) + 1e-6)


def gated_delta_rule_chunked(q, k, v, g, beta):
    b, h, t, dk = q.shape
    dv = v.shape[-1]
    n = t // CHUNK
    q = q * (dk ** -0.5)
    q, k, v = (a.reshape(b, h, n, CHUNK, a.shape[-1]) for a in (q, k, v))
    g = jnp.cumsum(g.reshape(b, h, n, CHUNK), axis=-1)
    beta = beta.reshape(b, h, n, CHUNK)
    causal = jnp.tril(jnp.ones((CHUNK, CHUNK), bool))
    strict = jnp.tril(jnp.ones((CHUNK, CHUNK), bool), -1)
    diff = g[..., :, None] - g[..., None, :]
    decay = jnp.where(causal, jnp.exp(jnp.where(causal, diff, 0.0)), 0.0)
    k_beta = k * beta[..., None]
    a_mat = jnp.where(strict, jnp.einsum('bhncd,bhnsd->bhncs', k_beta, k) * decay, 0.0)
    eye = jnp.eye(CHUNK, dtype=jnp.float32)
    rhs = jnp.concatenate([v * beta[..., None], k_beta * jnp.exp(g)[..., None]], axis=-1)
    sol = lax.linalg.triangular_solve(a_mat + eye, rhs, left_side=True, lower=True,
                                      unit_diagonal=True)
    u, w = sol[..., :dv], sol[..., dv:]
    qk = jnp.where(causal, jnp.einsum('bhncd,bhnsd->bhncs', q, k) * decay, 0.0)

    def step(state, xs):
        q_c, k_c, u_c, w_c, qk_c, g_c = xs
        v_new = u_c - jnp.einsum('bhck,bhkv->bhcv', w_c, state)
        o = (jnp.einsum('bhck,bhkv->bhcv', q_c * jnp.exp(g_c)[..., None], state)
             + jnp.einsum('bhcs,bhsv->bhcv', qk_c, v_new))
        g_last = g_c[..., -1:]
        state = (state * jnp.exp(g_last)[..., None]
                 + jnp.einsum('bhck,bhcv->bhkv', k_c * jnp.exp(g_last - g_c)[..., None], v_new))
        return state, o

    xs = tuple(jnp.moveaxis(a, 2, 0) for a in (q, k, u, w, qk, g))
    s0 = jnp.zeros((b, h, dk, dv), jnp.float32)
    _, o = lax.scan(step, s0, xs)
    return jnp.moveaxis(o, 0, 2).reshape(b, h, t, dv)


def s5_ssm(u, lam_re, lam_im, b_re, b_im, c_re, c_im, d, log_dt):
    uf = u.astype(jnp.float32)
    dt = jnp.exp(log_dt.astype(jnp.float32))[:, None]
    lr, li = lam_re.astype(jnp.float32), lam_im.astype(jnp.float32)
    mag = jnp.exp(lr * dt)
    ang = li * dt
    lb_re, lb_im = mag * jnp.cos(ang), mag * jnp.sin(ang)
    den = lr * lr + li * li
    nr, ni = lb_re - 1.0, lb_im
    coef_re = (nr * lr + ni * li) / den
    coef_im = (ni * lr - nr * li) / den
    br, bi = b_re.astype(jnp.float32), b_im.astype(jnp.float32)
    bb_re = coef_re[..., None] * br - coef_im[..., None] * bi
    bb_im = coef_re[..., None] * bi + coef_im[..., None] * br
    bu_re = jnp.einsum('gph,btgh->btgp', bb_re, uf)
    bu_im = jnp.einsum('gph,btgh->btgp', bb_im, uf)
    a_re = jnp.broadcast_to(lb_re, bu_re.shape)
    a_im = jnp.broadcast_to(lb_im, bu_re.shape)

    def combine(e1, e2):
        a1r, a1i, b1r, b1i = e1
        a2r, a2i, b2r, b2i = e2
        return (a1r * a2r - a1i * a2i,
                a1r * a2i + a1i * a2r,
                a2r * b1r - a2i * b1i + b2r,
                a2r * b1i + a2i * b1r + b2i)

    _, _, xr, xi = lax.associative_scan(combine, (a_re, a_im, bu_re, bu_im), axis=1)
    y = (jnp.einsum('ghp,btgp->btgh', c_re.astype(jnp.float32), xr)
         - jnp.einsum('ghp,btgp->btgh', c_im.astype(jnp.float32), xi)
         + d.astype(jnp.float32) * uf)
    return y


def hybrid_delta_ssm_mixer(xn, w_in, conv_qkv, a_log, dt_bias, onorm_g,
                           lam_re, lam_im, b_re, b_im, c_re, c_im, ssm_d, log_dt,
                           w_glu, b_glu, w_out):
    b, t, _ = xn.shape
    h = xn @ w_in
    qkv, gate, beta_logit, alpha_logit, u = jnp.split(
        h, [3 * WIDTH_A, 4 * WIDTH_A, 4 * WIDTH_A + N_HEADS_A, 4 * WIDTH_A + 2 * N_HEADS_A],
        axis=-1)
    qkv = jax.nn.silu(causal_dwconv(qkv, conv_qkv)).astype(jnp.float32)
    q, k, v = (a.reshape(b, t, N_HEADS_A, HEAD_DIM_A).transpose(0, 2, 1, 3)
               for a in jnp.split(qkv, 3, axis=-1))
    q, k = l2_normalize(q), l2_normalize(k)
    beta = jax.nn.sigmoid(beta_logit.astype(jnp.float32)).transpose(0, 2, 1)
    g = (-jnp.exp(a_log.astype(jnp.float32))
         * jax.nn.softplus(alpha_logit.astype(jnp.float32) + dt_bias.astype(jnp.float32))
         ).transpose(0, 2, 1)
    o = gated_delta_rule_chunked(q, k, v, g, beta).transpose(0, 2, 1, 3)
    o = rms_norm(o, onorm_g) * jax.nn.silu(
        gate.reshape(b, t, N_HEADS_A, HEAD_DIM_A).astype(jnp.float32))
    y_a = o.reshape(b, t, WIDTH_A)
    y = s5_ssm(u.reshape(b, t, N_GROUPS, SSM_GROUP), lam_re, lam_im, b_re, b_im,
               c_re, c_im, ssm_d, log_dt)
    y = jax.nn.gelu(y.reshape(b, t, SSM_WIDTH))
    y_b = y * jax.nn.sigmoid(y @ w_glu.astype(jnp.float32) + b_glu.astype(jnp.float32))
    mixed = jnp.concatenate([y_a, y_b], axis=-1).astype(xn.dtype)
    return mixed @ w_out


def multiscale_pool_mixer(xn, pool_w, pool_scale):
    b, t, d = xn.shape
    xf = xn.astype(jnp.float32)
    cs = jnp.pad(lax.cumsum(xf, axis=1), ((0, 0), (1, 0), (0, 0)))
    pos_count = jnp.arange(1, t + 1, dtype=jnp.float32)[:, None]
    outs = []
    for gi, win in enumerate(POOL_WINDOWS):
        sl = slice(gi * POOL_GROUP, (gi + 1) * POOL_GROUP)
        csg = cs[..., sl]
        lower = jnp.pad(csg, ((0, 0), (win, 0), (0, 0)))[:, 1:t + 1]
        mean = (csg[:, 1:] - lower) / jnp.minimum(pos_count, float(win))
        outs.append(jnp.einsum('btc,ce->bte', mean - xf[..., sl],
                               pool_w[gi].astype(jnp.float32)))
    return (jnp.concatenate(outs, axis=-1) * pool_scale.astype(jnp.float32)).astype(xn.dtype)


def mem_cross_attention(xn, mem_n, wq, wkv, wo):
    b, t, _ = xn.shape
    m = mem_n.shape[1]
    q = (xn @ wq).reshape(b, t, N_HEADS_X, HEAD_DIM_X)
    kv = (mem_n @ wkv).reshape(b, m, 2, N_HEADS_X, HEAD_DIM_X)
    k, v = kv[:, :, 0], kv[:, :, 1]
    s = jnp.einsum('bthd,bmhd->bhtm', q, k).astype(jnp.float32) * (HEAD_DIM_X ** -0.5)
    p = jax.nn.softmax(s, axis=-1).astype(v.dtype)
    o = jnp.einsum('bhtm,bmhd->bthd', p, v).reshape(b, t, N_HEADS_X * HEAD_DIM_X)
    return o @ wo


def conv_ffn(xn, w_up, conv_w, w_down):
    h = causal_dwconv(xn @ w_up, conv_w)
    gate, val = jnp.split(h, 2, axis=-1)
    return (jax.nn.silu(gate) * val) @ w_down


def setup_inputs(seed: int = 0) -> dict:
    key = jax.random.key(seed)
    ks = iter(jax.random.split(key, 48))

    def nrm(shape, scale):
        return jax.random.normal(next(ks), shape, jnp.float32) * scale

    def gain(shape):
        return 1.0 + nrm(shape, 0.05)

    E, O, L, D = N_EVEN, N_ODD, DEPTH, D_MODEL
    x = nrm((BATCH, SEQ, D), 1.0)
    mem = nrm((BATCH, MEM_LEN, D), 1.0)
    norm_mix_g = gain((L, D))
    norm_xa_g = gain((L, D))
    norm_ffn_g = gain((L, D))
    norm_mem_g = gain((D,))
    norm_final_g = gain((D,))
    w_in_ab = nrm((E, D, D_IN_AB), D ** -0.5)
    conv_qkv_a = nrm((E, CONV_A, 3 * WIDTH_A), CONV_A ** -0.5)
    a_log_a = jnp.log(jax.random.uniform(next(ks), (E, N_HEADS_A), jnp.float32, 1.0, 16.0))
    dt0 = jnp.exp(jax.random.uniform(next(ks), (E, N_HEADS_A), jnp.float32,
                                     math.log(1e-3), math.log(1e-1)))
    dt_bias_a = dt0 + jnp.log(-jnp.expm1(-dt0))
    onorm_g_a = gain((E, HEAD_DIM_A))
    ssm_lambda_re = -0.5 + nrm((E, N_GROUPS, SSM_STATE), 0.01)
    ssm_lambda_im = jnp.broadcast_to(
        math.pi * jnp.arange(SSM_STATE, dtype=jnp.float32), (E, N_GROUPS, SSM_STATE)
    ) + nrm((E, N_GROUPS, SSM_STATE), 0.001)
    ssm_b_re = nrm((E, N_GROUPS, SSM_STATE, SSM_GROUP), (2 * SSM_GROUP) ** -0.5)
    ssm_b_im = nrm((E, N_GROUPS, SSM_STATE, SSM_GROUP), (2 * SSM_GROUP) ** -0.5)
    ssm_c_re = nrm((E, N_GROUPS, SSM_GROUP, SSM_STATE), (2 * SSM_STATE) ** -0.5)
    ssm_c_im = nrm((E, N_GROUPS, SSM_GROUP, SSM_STATE), (2 * SSM_STATE) ** -0.5)
    ssm_d = nrm((E, N_GROUPS, SSM_GROUP), 1.0)
    ssm_log_dt = jax.random.uniform(next(ks), (E, N_GROUPS), jnp.float32,
                                    math.log(1e-3), math.log(1e-1))
    w_glu_b = nrm((E, SSM_WIDTH, SSM_WIDTH), SSM_WIDTH ** -0.5)
    b_glu_b = nrm((E, SSM_WIDTH), 0.01)
    w_out_ab = nrm((E, D_MIX_AB, D), D_MIX_AB ** -0.5)
    pool_w = nrm((O, len(POOL_WINDOWS), POOL_GROUP, POOL_GROUP), POOL_GROUP ** -0.5)
    pool_scale = 1.0 + nrm((O, D), 0.1)
    xa_wq = nrm((L, D, D), D ** -0.5)
    xa_wkv = nrm((L, D, 2 * D), D ** -0.5)
    xa_wo = nrm((L, D, D), D ** -0.5)
    ffn_w_up = nrm((L, D, 2 * D_FF), D ** -0.5)
    ffn_conv = nrm((L, CONV_FFN, 2 * D_FF), 0.3) + jnp.array([0.0, 0.0, 1.0], jnp.float32)[None, :, None]
    ffn_w_down = nrm((L, D_FF, D), D_FF ** -0.5)
    return {
        "x": x, "mem": mem,
        "norm_mix_g": norm_mix_g, "norm_xa_g": norm_xa_g, "norm_ffn_g": norm_ffn_g,
        "norm_mem_g": norm_mem_g, "norm_final_g": norm_final_g,
        "w_in_ab": w_in_ab, "conv_qkv_a": conv_qkv_a, "a_log_a": a_log_a,
        "dt_bias_a": dt_bias_a, "onorm_g_a": onorm_g_a,
        "ssm_lambda_re": ssm_lambda_re, "ssm_lambda_im": ssm_lambda_im,
        "ssm_b_re": ssm_b_re, "ssm_b_im": ssm_b_im, "ssm_c_re": ssm_c_re, "ssm_c_im": ssm_c_im,
        "ssm_d": ssm_d, "ssm_log_dt": ssm_log_dt, "w_glu_b": w_glu_b, "b_glu_b": b_glu_b,
        "w_out_ab": w_out_ab,
        "pool_w": pool_w, "pool_scale": pool_scale,
        "xa_wq": xa_wq, "xa_wkv": xa_wkv, "xa_wo": xa_wo,
        "ffn_w_up": ffn_w_up, "ffn_conv": ffn_conv, "ffn_w_down": ffn_w_down,
    }


def reference(x, mem, norm_mix_g, norm_xa_g, norm_ffn_g, norm_mem_g, norm_final_g,
              w_in_ab, conv_qkv_a, a_log_a, dt_bias_a, onorm_g_a,
              ssm_lambda_re, ssm_lambda_im, ssm_b_re, ssm_b_im, ssm_c_re, ssm_c_im,
              ssm_d, ssm_log_dt, w_glu_b, b_glu_b, w_out_ab,
              pool_w, pool_scale, xa_wq, xa_wkv, xa_wo,
              ffn_w_up, ffn_conv, ffn_w_down):
    mem_n = rms_norm(mem, norm_mem_g)
    for layer in range(DEPTH):
        xn = rms_norm(x, norm_mix_g[layer])
        if layer % 2 == 0:
            e = layer // 2
            mix = hybrid_delta_ssm_mixer(
                xn, w_in_ab[e], conv_qkv_a[e], a_log_a[e], dt_bias_a[e], onorm_g_a[e],
                ssm_lambda_re[e], ssm_lambda_im[e], ssm_b_re[e], ssm_b_im[e],
                ssm_c_re[e], ssm_c_im[e], ssm_d[e], ssm_log_dt[e],
                w_glu_b[e], b_glu_b[e], w_out_ab[e])
        else:
            o = layer // 2
            mix = multiscale_pool_mixer(xn, pool_w[o], pool_scale[o])
        x = x + mix.astype(x.dtype)
        x = x + mem_cross_attention(rms_norm(x, norm_xa_g[layer]), mem_n,
                                    xa_wq[layer], xa_wkv[layer], xa_wo[layer]).astype(x.dtype)
        x = x + conv_ffn(rms_norm(x, norm_ffn_g[layer]),
                         ffn_w_up[layer], ffn_conv[layer], ffn_w_down[layer]).astype(x.dtype)
    return rms_norm(x, norm_final_g)
```

```python
import math
import os
from contextlib import ExitStack
import numpy as np
import concourse.bass as bass
import concourse.mybir as mybir
from concourse.bass_utils import run_bass_kernel_spmd

F32 = mybir.dt.float32
BF16 = mybir.dt.bfloat16
I32 = mybir.dt.int32
AF = mybir.ActivationFunctionType
ALU = mybir.AluOpType

COMPUTE = ("pe", "act", "dve", "pool")
NDMASEM = 24
ARENA_WORDS = 53000

T = 2048
D = 1024
NTB = 4
DFF = 2816
NF = 22
MEM = 256
EPS = 1e-6
MAGIC = 12582912.0
TWO_PI = 2.0 * math.pi


class Dep:
    __slots__ = ("w", "r", "rd")

    def __init__(self):
        self.w = None
        self.r = {}
        self.rd = []


class Op:
    __slots__ = ("eng", "fn", "deps", "flag", "idx", "val", "snap", "dma", "dsem", "dval", "waits")

    def __init__(self, eng, fn, dma):
        self.eng = eng
        self.fn = fn
        self.dma = dma
        self.deps = []
        self.flag = False
        self.val = 0
        self.snap = None
        self.dsem = -1
        self.dval = 0
        self.waits = []


class Prog:
    def __init__(self, nc):
        self.nc = nc
        self.ops = {e: [] for e in COMPUTE + ("sp",)}
        self.all = []
        self.dma_count = 0
        self.dma_count2 = 0
        self.dma_last = [None] * NDMASEM
        self.dma_vals = [0] * NDMASEM
        self.pending = {e: [] for e in COMPUTE + ("sp",)}
        self.dmas_since_barrier = []

    def emit(self, eng, fn, reads=(), writes=(), dma=False):
        op = Op(eng, fn, dma)
        deps = {}
        for d in reads:
            if d.w is not None:
                deps[id(d.w)] = d.w
        for d in writes:
            if d.w is not None:
                deps[id(d.w)] = d.w
            for o in d.r.values():
                deps[id(o)] = o
            for o in d.rd:
                deps[id(o)] = o
        for o in self.pending[eng]:
            deps[id(o)] = o
        self.pending[eng] = []
        op.deps = list(deps.values())
        for d in reads:
            if dma:
                d.rd.append(op)
            else:
                d.r[eng] = op
        for d in writes:
            d.w = op
            d.r = {}
            d.rd = []
        op.idx = len(self.ops[eng])
        self.ops[eng].append(op)
        self.all.append(op)
        if dma:
            half = NDMASEM // 2
            if eng == "sp":
                s = self.dma_count % half
                self.dma_count += 1
            else:
                s = half + self.dma_count2 % half
                self.dma_count2 += 1
            prev = self.dma_last[s]
            if prev is not None:
                op.deps.append(prev)
            self.dma_vals[s] += 16
            op.dsem = s
            op.dval = self.dma_vals[s]
            self.dma_last[s] = op
            self.dmas_since_barrier.append(op)
        return op

    def barrier(self):
        lasts = []
        for e in self.ops:
            for o in reversed(self.ops[e]):
                if not o.dma:
                    lasts.append(o)
                    break
        lasts += list(self.dmas_since_barrier)
        self.dmas_since_barrier = []
        for e in self.pending:
            self.pending[e] = list(lasts)

    def resolve(self):
        seen = {e: {} for e in self.ops}
        for op in self.all:
            E = op.eng
            s = seen[E]
            for D_ in op.deps:
                if D_.dma:
                    key = ("d", D_.dsem)
                    v = D_.dval
                else:
                    key = D_.eng
                    v = D_.idx + 1
                    if D_.eng == "pe" and E == "pe" and not op.dma:
                        continue
                if s.get(key, 0) >= v:
                    continue
                op.waits.append(D_)
                if not D_.dma:
                    D_.flag = True
                s[key] = v
                if D_.snap is not None:
                    for k, vv in D_.snap.items():
                        if s.get(k, 0) < vv:
                            s[k] = vv
            op.snap = dict(s)
        for e in COMPUTE:
            c = 0
            for op in self.ops[e]:
                if op.dma:
                    continue
                if op.flag:
                    c += 1
                    op.val = c

    def run(self, stack):
        nc = self.nc
        self.resolve()
        sems = {e: stack.enter_context(nc.semaphore("s_" + e)) for e in COMPUTE}
        dsems = [stack.enter_context(nc.semaphore("d%d" % i)) for i in range(NDMASEM)]
        block = stack.enter_context(nc.Block())

        def body(ename):
            def f(e):
                for op in self.ops[ename]:
                    for D_ in op.waits:
                        if D_.dma:
                            e.wait_ge(dsems[D_.dsem], D_.dval)
                        else:
                            e.wait_ge(sems[D_.eng], D_.val)
                    ins = op.fn(e)
                    if op.dma:
                        ins.then_inc(dsems[op.dsem], 16)
                    elif op.flag:
                        ins.then_inc(sems[ename], 1)
            return f

        block.tensor(body("pe"))
        block.scalar(body("act"))
        block.vector(body("dve"))
        block.gpsimd(body("pool"))
        block.sync(body("sp"))


class Arena:
    def __init__(self, ap2d, nwords):
        self.ap = ap2d
        self.n = nwords
        self.off = 0
        self.marks = []

    def alloc(self, shape, dtype=F32, parts=128):
        n = int(np.prod(shape))
        words = n if dtype in (F32, I32) else (n + 1) // 2
        assert self.off + words <= self.n, ("arena overflow", self.off, words, self.n)
        a = self.ap[0:parts, self.off:self.off + words]
        self.off += words
        if dtype != F32:
            a = a.bitcast(dtype)
            if a.shape[-1] != n:
                a = a[:, 0:n]
        if len(shape) > 1:
            names = " ".join("d%d" % i for i in range(len(shape)))
            kw = {"d%d" % i: int(shape[i]) for i in range(len(shape))}
            a = a.rearrange("p (%s) -> p %s" % (names, names), **kw)
        return a

    def mark(self):
        self.marks.append(self.off)

    def release(self):
        self.off = self.marks.pop()


INPUT_SHAPES = [
    ("x", [T, D]), ("mem", [MEM, D]),
    ("norm_mix_g", [2, D]), ("norm_xa_g", [2, D]), ("norm_ffn_g", [2, D]),
    ("norm_mem_g", [D]), ("norm_final_g", [D]),
    ("w_in_ab", [1, D, 2568]), ("conv_qkv_a", [1, 4, 1536]), ("a_log_a", [1, 4]), ("dt_bias_a", [1, 4]),
    ("onorm_g_a", [1, 128]),
    ("ssm_lambda_re", [1, 32, 64]), ("ssm_lambda_im", [1, 32, 64]),
    ("ssm_b_re", [1, 32, 64, 16]), ("ssm_b_im", [1, 32, 64, 16]),
    ("ssm_c_re", [1, 32, 16, 64]), ("ssm_c_im", [1, 32, 16, 64]),
    ("ssm_d", [1, 32, 16]), ("ssm_log_dt", [1, 32]),
    ("w_glu_b", [1, 512, 512]), ("b_glu_b", [1, 512]), ("w_out_ab", [1, D, D]),
    ("pool_w", [1, 4, 256, 256]), ("pool_scale", [1, D]),
    ("xa_wq", [2, D, D]), ("xa_wkv", [2, D, 2 * D]), ("xa_wo", [2, D, D]),
    ("ffn_w_up", [2, D, 2 * DFF]), ("ffn_conv", [2, 3, 2 * DFF]), ("ffn_w_down", [2, DFF, D]),
]


class KB:
    def __init__(self, nc, st, dbg=None):
        self.nc = nc
        self.dbg = dbg or {}
        self.I = {n: nc.dram_tensor(n, s, F32, kind="ExternalInput").ap() for n, s in INPUT_SHAPES}
        self.out = nc.dram_tensor("out", [T, D], F32, kind="ExternalOutput").ap()
        self.dbg_out = {}
        for n, s in self.dbg.items():
            self.dbg_out[n] = nc.dram_tensor("dbg_" + n, s, F32, kind="ExternalOutput").ap()
        sb = st.enter_context(nc.sbuf_tensor("arena", [128, ARENA_WORDS], F32))
        ps = st.enter_context(nc.psum_tensor("ps", [128, 4096], F32))
        self.A = Arena(sb, ARENA_WORDS)
        self.P = Prog(nc)
        self.ps = ps
        self.bank = [ps[:, b * 512:(b + 1) * 512] for b in range(8)]
        self.dbank = [Dep() for _ in range(8)]
        self.alt = 0

    def mm(self, out, lhsT, rhs, start, stop, reads, writes, **kw):
        self.P.emit("pe", lambda e: e.matmul(out, lhsT=lhsT, rhs=rhs, start=start, stop=stop, **kw), reads, writes)

    def tr(self, out, in_, ident, reads, writes):
        self.P.emit("pe", lambda e: e.transpose(out, in_, ident), reads, writes)

    def act(self, out, in_, func, reads, writes, bias=None, scale=1.0, accum_out=None):
        kw = {}
        if bias is not None:
            kw["bias"] = bias
        if accum_out is not None:
            kw["accum_out"] = accum_out
        self.P.emit("act", lambda e: e.activation(out=out, in_=in_, func=func, scale=scale, **kw), reads, writes)

    def tt(self, eng, out, in0, in1, op, reads, writes):
        self.P.emit(eng, lambda e: e.tensor_tensor(out=out, in0=in0, in1=in1, op=op), reads, writes)

    def ts(self, eng, out, in0, s1, s2, op0, op1, reads, writes):
        if op1 is None:
            self.P.emit(eng, lambda e: e.tensor_scalar(out=out, in0=in0, scalar1=s1, scalar2=None, op0=op0), reads, writes)
        else:
            self.P.emit(eng, lambda e: e.tensor_scalar(out=out, in0=in0, scalar1=s1, scalar2=s2, op0=op0, op1=op1),
                        reads, writes)

    def stt(self, eng, out, in0, scalar, in1, op0, op1, reads, writes):
        eng = "dve"
        self.P.emit(eng, lambda e: e.scalar_tensor_tensor(out=out, in0=in0, scalar=scalar, in1=in1, op0=op0, op1=op1),
                    reads, writes)

    def cp(self, eng, out, in_, reads, writes):
        if eng == "act":
            self.P.emit("act", lambda e: e.activation(out=out, in_=in_, func=AF.Copy), reads, writes)
        else:
            self.P.emit(eng, lambda e: e.tensor_copy(out=out, in_=in_), reads, writes)

    def memset(self, eng, ap, val, writes):
        self.P.emit(eng, lambda e: e.memset(ap, val), (), writes)

    def dma(self, out, in_, reads, writes, q="sp", slow=False):
        if slow:
            self.P.emit(q, lambda e: e.dma_start(out=out, in_=in_, allow_slow_non_contiguous=True), reads, writes, dma=True)
        else:
            self.P.emit(q, lambda e: e.dma_start(out=out, in_=in_), reads, writes, dma=True)

    def ldw(self, out, in_, dep):
        self.P.emit("pool", lambda e: e.dma_start(out=out, in_=in_), (), [dep], dma=True)

    def ve(self):
        return "dve"

    def dump(self, name, ap, deps):
        if name in self.dbg_out:
            self.dma(self.dbg_out[name], ap, deps, [])

    def consts(self):
        A, P = self.A, self.P
        self.dconst = Dep()
        dc = [self.dconst]
        self.ones_f = A.alloc([128])
        self.negones = A.alloc([128])
        self.ones_bf = A.alloc([128], BF16)
        self.ident = A.alloc([128])
        self.tri = A.alloc([128])
        self.ml = A.alloc([128])
        self.mlneg = A.alloc([128])
        self.memset("pool", self.ones_f, 1.0, dc)
        self.zcol = A.alloc([1])
        self.memset("pool", self.zcol, 0.0, dc)
        self.memset("pool", self.negones, -1.0, dc)
        self.memset("pool", self.ones_bf, 1.0, dc)
        P.emit("pool", lambda e: e.affine_select(out=self.ident, in_=self.ones_f, pattern=[[1, 128]],
                                                 compare_op=ALU.is_equal, fill=0.0, base=0, channel_multiplier=-1),
               dc, dc)
        P.emit("pool", lambda e: e.affine_select(out=self.tri, in_=self.ones_f, pattern=[[1, 128]],
                                                 compare_op=ALU.is_ge, fill=0.0, base=0, channel_multiplier=-1),
               dc, dc)
        P.emit("pool", lambda e: e.affine_select(out=self.ml, in_=self.ones_f, pattern=[[-1, 128]],
                                                 compare_op=ALU.is_ge, fill=0.0, base=0, channel_multiplier=1),
               dc, dc)
        P.emit("pool", lambda e: e.affine_select(out=self.mlneg, in_=self.negones, pattern=[[-1, 128]],
                                                 compare_op=ALU.is_gt, fill=0.0, base=0, channel_multiplier=1),
               dc, dc)
        self.gT = A.alloc([8, 8])
        I = self.I
        srcs = [I["norm_mix_g"][0], I["norm_mix_g"][1], I["norm_xa_g"][0], I["norm_xa_g"][1],
                I["norm_ffn_g"][0], I["norm_ffn_g"][1], I["norm_final_g"], I["norm_mem_g"]]
        for i, s in enumerate(srcs):
            self.dma(self.gT[:, i, :], s.rearrange("(c p) -> p c", p=128), [], dc, slow=True)
        self.sq = A.alloc([8, 512], BF16)
        self.dsq = Dep()
        self.rt = A.alloc([512])
        self.drt = Dep()
        self.rs = A.alloc([512])
        self.drs = Dep()

    def rmsnorm(self, src, dsrc, gi, dst, ddst, nblk, bs, bankid=0):
        for tb in range(nblk):
            sl = slice(tb * bs, (tb + 1) * bs)
            rd = [dsrc[c][tb] for c in range(8)]
            self.act(self.sq[:, :, 0:bs], src[:, :, sl], AF.Square, rd, [self.dsq])
            bk, dbk = self.bank[bankid], self.dbank[bankid]
            for c in range(8):
                self.mm(bk[:, 0:bs], self.ones_bf, self.sq[:, c, 0:bs], c == 0, c == 7,
                        [self.dsq, self.dconst], [dbk])
            self.act(self.rt[:, 0:bs], bk[:, 0:bs], AF.Sqrt, [dbk], [self.drt], bias=self.epsb, scale=1.0 / D)
            self.P.emit("dve", lambda e, o=self.rs[:, 0:bs], i=self.rt[:, 0:bs]: e.reciprocal(out=o, in_=i),
                        [self.drt], [self.drs])
            for c in range(8):
                self.stt(self.ve(), dst[:, c, sl], src[:, c, sl], self.gT[:, gi, c:c + 1], self.rs[:, 0:bs],
                         ALU.mult, ALU.mult, [dsrc[c][tb], self.drs, self.dconst], [ddst[c][tb]])

    def load_T(self, src_dram, ntile, dstT, ddst, tiles_per_blk):
        A = self.A
        A.mark()
        xin = [A.alloc([D]), A.alloc([D])]
        dxin = [Dep(), Dep()]
        for i in range(ntile):
            b = i % 2
            self.dma(xin[b], src_dram[i * 128:(i + 1) * 128, :], [], [dxin[b]])
            for half in range(2):
                bid = (i % 2) * 2 + half
                bk, dbk = self.bank[bid], self.dbank[bid]
                for cc in range(4):
                    c = half * 4 + cc
                    self.tr(bk[:, cc * 128:(cc + 1) * 128], xin[b][:, c * 128:(c + 1) * 128], self.ident,
                            [dxin[b], self.dconst], [dbk])
                eng = "act" if half == 0 else "dve"
                self.cp(eng, dstT[:, half * 4:half * 4 + 4, i * 128:(i + 1) * 128],
                        bk.rearrange("p (c t) -> p c t", c=4), [dbk],
                        [ddst[c][i // tiles_per_blk] for c in range(half * 4, half * 4 + 4)])
        self.P.barrier()
        A.release()

    def proj_tile(self, wt, dwt, srcT, dsrc, nk, tb, bk, dbk, bs=512, koff=0):
        sl = slice(tb * bs, (tb + 1) * bs)
        for k in range(nk):
            self.mm(bk[:, 0:bs], wt[:, k, :], srcT[:, koff + k, sl], k == 0, k == nk - 1,
                    [dwt, dsrc[koff + k][tb]], [dbk])

    def xattn(self, layer):
        A, I = self.A, self.I
        A.mark()
        xT, dxT, xnT, dxn = self.xT, self.dxT, self.xnT, self.dxn
        self.rmsnorm(xT, dxT, 2 + layer, xnT, dxn, NTB, 512)
        memT = A.alloc([8, MEM])
        dmemT = [[Dep()] for _ in range(8)]
        self.load_T(I["mem"], 2, memT, dmemT, 2)
        mnT = A.alloc([8, MEM], BF16)
        dmn = [[Dep()] for _ in range(8)]
        self.rmsnorm(memT, dmemT, 7, mnT, dmn, 1, MEM)
        wkv = I["xa_wkv"][layer]
        kT = A.alloc([8, MEM], BF16)
        dkT = [Dep() for _ in range(8)]
        wt = [A.alloc([8, 128], BF16), A.alloc([8, 128], BF16)]
        dwt = [Dep(), Dep()]
        for f in range(8):
            b = f % 2
            self.ldw(wt[b], wkv[:, f * 128:(f + 1) * 128].rearrange("(k p) f -> p k f", p=128), dwt[b])
            bk, dbk = self.bank[4 + b], self.dbank[4 + b]
            for k in range(8):
                self.mm(bk[:, 0:MEM], wt[b][:, k, :], mnT[:, k, :], k == 0, k == 7, [dwt[b], dmn[k][0]], [dbk])
            self.cp("act" if b else "dve", kT[:, f, :], bk[:, 0:MEM], [dbk], [dkT[f]])
        vv = A.alloc([2, D], BF16)
        dvv = [[Dep(), Dep()] for _ in range(2)]
        wv = [A.alloc([8, 512], BF16), A.alloc([8, 512], BF16)]
        dwv = [Dep(), Dep()]
        for n in range(2):
            self.ldw(wv[n], wkv[:, D + n * 512:D + (n + 1) * 512].rearrange("(k p) f -> p k f", p=128), dwv[n])
            for mc in range(2):
                bk, dbk = self.bank[6 + mc], self.dbank[6 + mc]
                for k in range(8):
                    self.mm(bk, mnT[:, k, mc * 128:(mc + 1) * 128], wv[n][:, k, :], k == 0, k == 7,
                            [dwv[n], dmn[k][0]], [dbk])
                self.cp("act" if mc else "dve", vv[:, mc, n * 512:(n + 1) * 512], bk, [dbk], [dvv[mc][n]])
        oT = A.alloc([8, T], BF16)
        doT = [[Dep() for _ in range(NTB)] for _ in range(8)]
        qT = A.alloc([2, T], BF16)
        dq = [[Dep() for _ in range(NTB)] for _ in range(2)]
        pT = A.alloc([2, 512], BF16)
        dpT = [Dep(), Dep()]
        rinv = A.alloc([512])
        drinv = Dep()
        wq = I["xa_wq"][layer]
        for h in range(4):
            for dc in range(2):
                f = 2 * h + dc
                b = f % 2
                self.ldw(wt[b], wq[:, f * 128:(f + 1) * 128].rearrange("(k p) f -> p k f", p=128), dwt[b])
                for tb in range(NTB):
                    bid = (f * NTB + tb) % 2
                    bk, dbk = self.bank[bid], self.dbank[bid]
                    self.proj_tile(wt[b], dwt[b], xnT, dxn, 8, tb, bk, dbk)
                    self.cp("act" if tb % 2 else "dve", qT[:, dc, tb * 512:(tb + 1) * 512], bk, [dbk], [dq[dc][tb]])
            for tb in range(NTB):
                sl = slice(tb * 512, (tb + 1) * 512)
                for mc in range(2):
                    bk, dbk = self.bank[2 + mc], self.dbank[2 + mc]
                    for dc in range(2):
                        self.mm(bk, kT[:, 2 * h + dc, mc * 128:(mc + 1) * 128], qT[:, dc, sl], dc == 0, dc == 1,
                                [dkT[2 * h + dc], dq[dc][tb]], [dbk])
                    self.act(pT[:, mc, :], bk, AF.Exp, [dbk], [dpT[mc]], scale=1.0 / 16.0)
                bs_, dbs = self.bank[4], self.dbank[4]
                for mc in range(2):
                    self.mm(bs_, self.ones_bf, pT[:, mc, :], mc == 0, mc == 1, [dpT[mc], self.dconst], [dbs])
                self.P.emit("dve", lambda e, o=rinv, i=bs_: e.reciprocal(out=o, in_=i), [dbs], [drinv])
                for dvc in range(2):
                    bk, dbk = self.bank[5 + dvc], self.dbank[5 + dvc]
                    col = h * 256 + dvc * 128
                    for mc in range(2):
                        self.mm(bk, vv[:, mc, col:col + 128], pT[:, mc, :], mc == 0, mc == 1,
                                [dvv[mc][col // 512], dpT[mc]], [dbk])
                    self.tt("dve", oT[:, 2 * h + dvc, sl], bk, rinv, ALU.mult, [dbk, drinv], [doT[2 * h + dvc][tb]])
        wo = I["xa_wo"][layer]
        for f in range(8):
            b = f % 2
            self.ldw(wt[b], wo[:, f * 128:(f + 1) * 128].rearrange("(k p) f -> p k f", p=128), dwt[b])
            for tb in range(NTB):
                bid = (f * NTB + tb) % 2
                bk, dbk = self.bank[bid], self.dbank[bid]
                self.proj_tile(wt[b], dwt[b], oT, doT, 8, tb, bk, dbk)
                sl = slice(tb * 512, (tb + 1) * 512)
                self.tt("dve", xT[:, f, sl], bk, xT[:, f, sl], ALU.add, [dbk, dxT[f][tb]], [dxT[f][tb]])
        self.P.barrier()
        A.release()

    def ffn(self, layer):
        A, I = self.A, self.I
        A.mark()
        xT, dxT, xnT, dxn = self.xT, self.dxT, self.xnT, self.dxn
        self.rmsnorm(xT, dxT, 4 + layer, xnT, dxn, NTB, 512)
        wup = I["ffn_w_up"][layer]
        wdn = I["ffn_w_down"][layer]
        cw = A.alloc([3, 2 * NF])
        dcw = Dep()
        for i in range(3):
            self.dma(cw[:, i, :], I["ffn_conv"][layer, i].rearrange("(c p) -> p c", p=128), [], [dcw], slow=True)
        actT = A.alloc([NF, 1024], BF16)
        dact = [[Dep(), Dep()] for _ in range(NF)]
        halo = A.alloc([2 * NF, 2])
        dhalo = [Dep() for _ in range(2 * NF)]
        hb = [A.alloc([1026]) for _ in range(4)]
        dhb = [Dep() for _ in range(4)]
        acc = [A.alloc([1024]) for _ in range(2)]
        dacc = [Dep() for _ in range(2)]
        ptmp = A.alloc([1024])
        dptmp = Dep()
        wu = [A.alloc([2, 8, 128], BF16), A.alloc([2, 8, 128], BF16)]
        dwu = [Dep(), Dep()]
        wd = [A.alloc([NF, 128], BF16), A.alloc([NF, 128], BF16)]
        dwd = [Dep(), Dep()]
        it = 0
        for th in range(2):
            for f in range(NF):
                b = f % 2
                for gv in range(2):
                    col = gv * DFF + f * 128
                    self.ldw(wu[b][:, gv], wup[:, col:col + 128].rearrange("(k p) f -> p k f", p=128), dwu[b])
                for gv in range(2):
                    ch = gv * NF + f
                    u = (it % 2) * 2 + gv
                    for t2 in range(2):
                        bid = gv * 4 + (it % 2) * 2 + t2
                        bk, dbk = self.bank[bid], self.dbank[bid]
                        self.proj_tile(wu[b][:, gv], dwu[b], xnT, dxn, 8, th * 2 + t2, bk, dbk)
                        self.cp("act", hb[u][:, 2 + t2 * 512:2 + (t2 + 1) * 512], bk, [dbk], [dhb[u]])
                    if th == 0:
                        self.memset("dve", hb[u][:, 0:2], 0.0, [dhb[u]])
                    else:
                        self.cp("dve", hb[u][:, 0:2], halo[:, ch, :], [dhalo[ch]], [dhb[u]])
                    if th == 0:
                        self.cp("dve", halo[:, ch, :], hb[u][:, 1024:1026], [dhb[u]], [dhalo[ch]])
                    self.ts("dve", acc[gv], hb[u][:, 0:1024], cw[:, 0, ch:ch + 1], None, ALU.mult, None,
                            [dhb[u], dcw], [dacc[gv]])
                    self.stt("dve", acc[gv], hb[u][:, 1:1025], cw[:, 1, ch:ch + 1], acc[gv], ALU.mult, ALU.add,
                             [dhb[u], dcw, dacc[gv]], [dacc[gv]])
                    self.stt("dve", acc[gv], hb[u][:, 2:1026], cw[:, 2, ch:ch + 1], acc[gv], ALU.mult, ALU.add,
                             [dhb[u], dcw, dacc[gv]], [dacc[gv]])
                self.act(acc[0], acc[0], AF.Silu, [dacc[0]], [dacc[0]])
                self.tt("dve", actT[:, f, :], acc[0], acc[1], ALU.mult, [dacc[0], dacc[1]],
                        [dact[f][0], dact[f][1]])
                it += 1
            for ft in range(8):
                b = ft % 2
                self.ldw(wd[b], wdn[:, ft * 128:(ft + 1) * 128].rearrange("(k p) f -> p k f", p=128), dwd[b])
                for t2 in range(2):
                    tb = th * 2 + t2
                    bid = (ft * 2 + t2) % 4
                    bk, dbk = self.bank[bid], self.dbank[bid]
                    sl2 = slice(t2 * 512, (t2 + 1) * 512)
                    for k in range(NF):
                        self.mm(bk, wd[b][:, k, :], actT[:, k, sl2], k == 0, k == NF - 1, [dwd[b], dact[k][t2]], [dbk])
                    sl = slice(tb * 512, (tb + 1) * 512)
                    self.tt("dve", xT[:, ft, sl], bk, xT[:, ft, sl], ALU.add, [dbk, dxT[ft][tb]], [dxT[ft][tb]])
        self.P.barrier()
        A.release()

    def poolmix(self, layer):
        A, I = self.A, self.I
        A.mark()
        xT, dxT, xnT, dxn = self.xT, self.dxT, self.xnT, self.dxn
        self.rmsnorm(xT, dxT, 0 + layer, xnT, dxn, NTB, 512)
        PAD = 16
        bufs = [A.alloc([PAD + T]) for _ in range(3)]
        dbuf = [Dep() for _ in range(3)]
        for b in range(3):
            self.memset("pool", bufs[b][:, 0:PAD], 0.0, [dbuf[b]])
        diffT = A.alloc([2, T], BF16)
        ddiff = [[Dep() for _ in range(NTB)] for _ in range(2)]
        pw = A.alloc([2, 256], BF16)
        dpw = Dep()
        psc = A.alloc([8])
        dpsc = Dep()
        self.dma(psc, I["pool_scale"][0].rearrange("(c p) -> p c", p=128), [], [dpsc], slow=True)
        cnt = A.alloc([16])
        dcnt = Dep()
        invc = A.alloc([4, 16])
        self.P.emit("pool", lambda e: e.iota(cnt, pattern=[[1, 16]], base=1, channel_multiplier=0,
                                             allow_small_or_imprecise_dtypes=True), [], [dcnt])
        for gi in range(4):
            w = 2 << gi
            self.ts("dve", invc[:, gi, :], cnt, float(w), None, ALU.min, None, [dcnt], [dcnt])
        self.P.emit("dve", lambda e: e.reciprocal(out=invc, in_=invc), [dcnt], [dcnt])
        for gi in range(4):
            w = 2 << gi
            self.ldw(pw, I["pool_w"][0, gi].rearrange("(k p) f -> p k f", p=128), dpw)
            for cc in range(2):
                c = 2 * gi + cc
                rd = [dxn[c][tb] for tb in range(NTB)]
                cur = 0
                self.cp("act", bufs[0][:, PAD:], xnT[:, c, :], rd, [dbuf[0]])
                s = 1
                src = 0
                while s < w:
                    nxt = 1 if src != 1 else 2
                    eng = self.ve()
                    self.tt(eng, bufs[nxt][:, PAD:], bufs[src][:, PAD:], bufs[src][:, PAD - s:PAD - s + T], ALU.add,
                            [dbuf[src]], [dbuf[nxt]])
                    src = nxt
                    s *= 2
                wr = [ddiff[cc][tb] for tb in range(NTB)]
                self.stt("dve", diffT[:, cc, :], bufs[src][:, PAD:], 1.0 / w, bufs[0][:, PAD:], ALU.mult, ALU.subtract,
                         [dbuf[src], dbuf[0]], wr)
                tmp = bufs[2 if src != 2 else 1]
                dtmp = dbuf[2 if src != 2 else 1]
                self.tt("dve", tmp[:, PAD:PAD + w], bufs[src][:, PAD:PAD + w], invc[:, gi, 0:w], ALU.mult,
                        [dbuf[src], dcnt], [dtmp])
                self.tt("dve", diffT[:, cc, 0:w], tmp[:, PAD:PAD + w], bufs[0][:, PAD:PAD + w], ALU.subtract,
                        [dtmp, dbuf[0]], [ddiff[cc][0]])
            for e_ in range(2):
                f = 2 * gi + e_
                for tb in range(NTB):
                    bid = (f * NTB + tb) % 4
                    bk, dbk = self.bank[bid], self.dbank[bid]
                    sl = slice(tb * 512, (tb + 1) * 512)
                    for cc in range(2):
                        self.mm(bk, pw[:, cc, e_ * 128:(e_ + 1) * 128], diffT[:, cc, sl], cc == 0, cc == 1,
                                [dpw, ddiff[cc][tb]], [dbk])
                    self.stt("dve", xT[:, f, sl], bk, psc[:, f:f + 1], xT[:, f, sl], ALU.mult, ALU.add,
                             [dbk, dpsc, dxT[f][tb]], [dxT[f][tb]])
        self.P.barrier()
        A.release()

    def final(self):
        A = self.A
        A.mark()
        xT, dxT = self.xT, self.dxT
        xf = A.alloc([8, 512])
        dxf = [[Dep()] for _ in range(8)]
        ot = [A.alloc([D]), A.alloc([D])]
        dot = [Dep(), Dep()]
        n = 0
        for tb in range(NTB):
            srcv = xT[:, :, tb * 512:(tb + 1) * 512]
            dsv = [[dxT[c][tb]] for c in range(8)]
            self.rmsnorm(srcv, dsv, 6, xf, dxf, 1, 512)
            for j in range(4):
                b = n % 2
                for half in range(2):
                    bid = 4 + b * 2 + half
                    bk, dbk = self.bank[bid], self.dbank[bid]
                    for cc in range(4):
                        c = half * 4 + cc
                        self.tr(bk[:, cc * 128:(cc + 1) * 128], xf[:, c, j * 128:(j + 1) * 128], self.ident,
                                [dxf[c][0], self.dconst], [dbk])
                    self.cp("act" if half else "dve", ot[b][:, half * 512:(half + 1) * 512], bk, [dbk], [dot[b]])
                i = tb * 4 + j
                self.dma(self.out[i * 128:(i + 1) * 128, :], ot[b], [dot[b]], [])
                n += 1
        A.release()

    def build(self, layers=(0, 1), parts=("mix", "xa", "ffn"), sub=("s5", "dn")):
        A = self.A
        self.sub = sub
        self.consts()
        self.epsb = EPS
        self.halfpi = math.pi / 2.0
        self.xT = A.alloc([8, T])
        self.dxT = [[Dep() for _ in range(NTB)] for _ in range(8)]
        self.xn_raw = A.alloc([8 * T // 2])
        self.xnT = self.xn_raw.bitcast(BF16).rearrange("p (c t) -> p c t", c=8)
        self.dxn = [[Dep() for _ in range(NTB)] for _ in range(8)]
        self.load_T(self.I["x"], 16, self.xT, self.dxT, 4)
        for layer in layers:
            if "mix" in parts:
                if layer == 0:
                    self.hybrid(layer, self.sub)
                else:
                    self.poolmix(layer)
            if "xa" in parts:
                self.xattn(layer)
            if "ffn" in parts:
                self.ffn(layer)
        self.final()
        self.P.barrier()
        self.P.emit("sp", lambda e: e.nop())


    def s5(self, ybT, dyb):
        A, I, P = self.A, self.I, self.P
        A.mark()
        B2 = Arena(self.xn_raw, 8 * T // 2)
        xT, dxT, xnT, dxn = self.xT, self.dxT, self.xnT, self.dxn
        win = I["w_in_ab"][0]
        Q = 512
        uT = A.alloc([4, T], BF16)
        du = [[Dep() for _ in range(NTB)] for _ in range(4)]
        wt = [A.alloc([8, 128], BF16), A.alloc([8, 128], BF16)]
        dwt = [Dep(), Dep()]
        for c in range(4):
            b = c % 2
            self.ldw(wt[b], win[:, 2056 + c * 128:2056 + (c + 1) * 128].rearrange("(k p) f -> p k f", p=128), dwt[b])
            for tb in range(NTB):
                bid = (c * NTB + tb) % 2
                bk, dbk = self.bank[bid], self.dbank[bid]
                self.proj_tile(wt[b], dwt[b], xnT, dxn, 8, tb, bk, dbk)
                self.cp("act" if tb % 2 else "dve", uT[:, c, tb * 512:(tb + 1) * 512], bk, [dbk], [du[c][tb]])
        P.barrier()
        dp = Dep()
        DP = [dp]
        names = ["lre", "lim", "ldt", "dt", "mag", "ang", "yv", "r", "fr", "afr", "sn", "cs", "lbr", "lbi",
                 "den", "t1", "t2", "cre", "cim", "ncim", "ncre", "fo1", "fo2", "fo3"]
        t = {n: A.alloc([16]) for n in names}
        for gh in range(2):
            ps_ = slice(gh * 64, (gh + 1) * 64)
            self.dma(t["lre"][ps_, :], I["ssm_lambda_re"][0].rearrange("(j gh) p -> gh p j", gh=2)[gh], [], DP, slow=True)
            self.dma(t["lim"][ps_, :], I["ssm_lambda_im"][0].rearrange("(j gh) p -> gh p j", gh=2)[gh], [], DP, slow=True)
            self.dma(t["ldt"][ps_, :], I["ssm_log_dt"][0].rearrange("(j gh) -> gh j", gh=2)[gh].partition_broadcast(64),
                     [], DP, slow=True)
        self.act(t["dt"], t["ldt"], AF.Exp, DP, DP)
        self.tt("dve", t["t1"], t["lre"], t["dt"], ALU.mult, DP, DP)
        self.act(t["mag"], t["t1"], AF.Exp, DP, DP)
        self.tt("dve", t["ang"], t["lim"], t["dt"], ALU.mult, DP, DP)
        self.ts("dve", t["yv"], t["ang"], 1.0 / TWO_PI, None, ALU.mult, None, DP, DP)
        self.ts("dve", t["r"], t["yv"], MAGIC, MAGIC, ALU.add, ALU.subtract, DP, DP)
        self.tt("dve", t["fr"], t["yv"], t["r"], ALU.subtract, DP, DP)
        self.act(t["afr"], t["fr"], AF.Abs, DP, DP)
        for q_ in (1, 2, 3):
            self.ts("dve", t["fo%d" % q_], t["fr"], float(q_ * 512), None, ALU.mult, None, DP, DP)
        self.act(t["sn"], t["fr"], AF.Sin, DP, DP, scale=TWO_PI)
        self.act(t["cs"], t["afr"], AF.Sin, DP, DP, scale=-TWO_PI, bias=self.halfpi)
        self.tt("dve", t["lbr"], t["mag"], t["cs"], ALU.mult, DP, DP)
        self.tt("dve", t["lbi"], t["mag"], t["sn"], ALU.mult, DP, DP)
        self.tt("dve", t["den"], t["lre"], t["lre"], ALU.mult, DP, DP)
        self.tt("dve", t["t1"], t["lim"], t["lim"], ALU.mult, DP, DP)
        self.tt("dve", t["den"], t["den"], t["t1"], ALU.add, DP, DP)
        P.emit("dve", lambda e: e.reciprocal(out=t["den"], in_=t["den"]), DP, DP)
        self.ts("dve", t["lbr"], t["lbr"], -1.0, None, ALU.add, None, DP, DP)
        self.tt("dve", t["t1"], t["lbr"], t["lre"], ALU.mult, DP, DP)
        self.tt("dve", t["t2"], t["lbi"], t["lim"], ALU.mult, DP, DP)
        self.tt("dve", t["t1"], t["t1"], t["t2"], ALU.add, DP, DP)
        self.tt("dve", t["cre"], t["t1"], t["den"], ALU.mult, DP, DP)
        self.tt("dve", t["t1"], t["lbi"], t["lre"], ALU.mult, DP, DP)
        self.tt("dve", t["t2"], t["lbr"], t["lim"], ALU.mult, DP, DP)
        self.tt("dve", t["t1"], t["t1"], t["t2"], ALU.subtract, DP, DP)
        self.tt("dve", t["cim"], t["t1"], t["den"], ALU.mult, DP, DP)
        self.ts("dve", t["ncim"], t["cim"], -1.0, None, ALU.mult, None, DP, DP)
        self.ts("dve", t["ncre"], t["cre"], -1.0, None, ALU.mult, None, DP, DP)
        cn = [A.alloc([16, 16]), A.alloc([16, 16])]
        bn = [A.alloc([16, 16]), A.alloc([16, 16])]
        for gh in range(2):
            ps_ = slice(gh * 64, (gh + 1) * 64)
            for ri, nm in enumerate(("ssm_c_re", "ssm_c_im")):
                src = I[nm][0].rearrange("(j gh) h p -> gh j p h", gh=2)[gh]
                for j_ in range(16):
                    self.dma(cn[ri][ps_, j_, :], src[j_], [], DP, slow=True)
            for ri, nm in enumerate(("ssm_b_re", "ssm_b_im")):
                self.dma(bn[ri][ps_], I[nm][0].rearrange("(j gh) p h -> gh p j h", gh=2)[gh], [], DP, slow=True)
        ce = [A.alloc([16, 16]), A.alloc([16, 16])]
        tmpc = A.alloc([16, 16])

        def bc(a):
            return a.unsqueeze(2).to_broadcast([128, 16, 16])
        self.tt("dve", ce[0], cn[0], bc(t["cre"]), ALU.mult, DP, DP)
        self.tt("dve", tmpc, cn[1], bc(t["ncim"]), ALU.mult, DP, DP)
        self.tt("dve", ce[0], ce[0], tmpc, ALU.add, DP, DP)
        self.tt("dve", ce[1], cn[0], bc(t["ncim"]), ALU.mult, DP, DP)
        self.tt("dve", tmpc, cn[1], bc(t["ncre"]), ALU.mult, DP, DP)
        self.tt("dve", ce[1], ce[1], tmpc, ALU.add, DP, DP)
        CE = A.alloc([16, 2, 32], BF16)
        self.memset("pool", CE, 0.0, DP)
        BP = A.alloc([2, 16, 32])
        self.memset("pool", BP, 0.0, DP)
        for gh in range(2):
            ps_ = slice(gh * 64, (gh + 1) * 64)
            for ri in range(2):
                self.cp("dve", CE[ps_, :, ri, gh * 16:(gh + 1) * 16], ce[ri][ps_], DP, DP)
                self.cp("dve", BP[ps_, ri, :, gh * 16:(gh + 1) * 16], bn[ri][ps_], DP, DP)
        BT = A.alloc([4, 2, 128], BF16)
        for c in range(4):
            for ri in range(2):
                bk, dbk = self.bank[2 + ri], self.dbank[2 + ri]
                self.tr(bk[:, 0:128], BP[:, ri, 4 * c:4 * c + 4, :].rearrange("p a b -> p (a b)"), self.ident,
                        DP + [self.dconst], [dbk])
                self.cp("dve", BT[:, c, ri, :], bk[:, 0:128], [dbk], DP)
        dvec = A.alloc([4])
        bglu = A.alloc([4])
        self.dma(dvec, I["ssm_d"][0].rearrange("g h -> (g h)").rearrange("(c p) -> p c", p=128), [], DP, slow=True)
        self.dma(bglu, I["b_glu_b"][0].rearrange("(c p) -> p c", p=128), [], DP, slow=True)
        iot = A.alloc([Q])
        P.emit("pool", lambda e: e.iota(iot, pattern=[[1, Q]], base=0, channel_multiplier=0,
                                        allow_small_or_imprecise_dtypes=True), [], DP)
        onesq = A.alloc([Q])
        self.memset("pool", onesq, 1.0, DP)
        def sc():
            return B2.alloc([Q]), Dep()
        yv, dyv = sc()
        fq, dfq = sc()
        TS, dTS = sc()
        TC, dTC = sc()
        bu = [sc(), sc()]
        pa = [sc() for _ in range(4)]
        magt, dmagt = sc()
        W = [sc(), sc()]
        carry = B2.alloc([2])
        dcarry = Dep()
        XR = B2.alloc([Q], BF16)
        XI = B2.alloc([Q], BF16)
        dX = [Dep(), Dep()]
        yp, dyp = B2.alloc([Q]), Dep()
        ygT = A.alloc([4, T], BF16)
        dyg = [[Dep() for _ in range(NTB)] for _ in range(4)]
        g1, dg1 = A.alloc([Q]), Dep()
        g2, dg2 = A.alloc([Q]), Dep()
        for c in range(4):
            for s_ in range(4):
                j = 4 * c + s_
                rows = slice(32 * s_, 32 * s_ + 32)
                self.ts("pool", magt, onesq, t["mag"][:, j:j + 1], None, ALU.mult, None, DP, [dmagt])
                for tb in range(NTB):
                    sl = slice(tb * Q, (tb + 1) * Q)
                    if tb == 0:
                        self.ts("dve", yv, iot, t["fr"][:, j:j + 1], None, ALU.mult, None, DP, [dyv])
                    else:
                        self.ts("dve", yv, iot, t["fr"][:, j:j + 1], t["fo%d" % tb][:, j:j + 1], ALU.mult, ALU.add,
                                DP, [dyv])
                    self.ts("dve", fq, yv, MAGIC, MAGIC, ALU.add, ALU.subtract, [dyv], [dfq])
                    self.tt("pool", fq, yv, fq, ALU.subtract, [dyv, dfq], [dfq])
                    self.act(yv, fq, AF.Abs, [dfq], [dyv])
                    self.act(TS, fq, AF.Sin, [dfq], [dTS], scale=TWO_PI)
                    self.act(TC, yv, AF.Sin, [dyv], [dTC], scale=-TWO_PI, bias=self.halfpi)
                    for ri in range(2):
                        bk, dbk = self.bank[ri], self.dbank[ri]
                        self.mm(bk, BT[rows, c, ri, :], uT[rows, c, sl], True, True, DP + [du[c][tb]], [dbk],
                                tile_position=(32 * s_, 0))
                        self.cp("act", bu[ri][0], bk, [dbk], [bu[ri][1]])
                    self.tt("dve", pa[0][0], TC, bu[0][0], ALU.mult, [dTC, bu[0][1]], [pa[0][1]])
                    self.tt("dve", pa[1][0], TS, bu[1][0], ALU.mult, [dTS, bu[1][1]], [pa[1][1]])
                    self.tt("pool", pa[2][0], TC, bu[1][0], ALU.mult, [dTC, bu[1][1]], [pa[2][1]])
                    self.tt("pool", pa[3][0], TS, bu[0][0], ALU.mult, [dTS, bu[0][1]], [pa[3][1]])
                    FR, dFR = pa[0]
                    FI, dFI = pa[2]
                    self.tt("dve", FR, pa[0][0], pa[1][0], ALU.add, [pa[0][1], pa[1][1]], [dFR])
                    self.tt("pool", FI, pa[2][0], pa[3][0], ALU.subtract, [pa[2][1], pa[3][1]], [dFI])
                    for ri, (F_, dF_) in enumerate(((FR, dFR), (FI, dFI))):
                        Wt, dW = W[ri]
                        if tb == 0:
                            P.emit("dve", lambda e, o=Wt, d0=magt, d1=F_: e.tensor_tensor_scan(
                                out=o, data0=d0, data1=d1, initial=0.0, op0=ALU.mult, op1=ALU.add),
                                [dmagt, dF_], [dW])
                        else:
                            P.emit("dve", lambda e, o=Wt, d0=magt, d1=F_, ini=carry[:, ri:ri + 1]: e.tensor_tensor_scan(
                                out=o, data0=d0, data1=d1, initial=ini, op0=ALU.mult, op1=ALU.add),
                                [dmagt, dF_, dcarry], [dW])
                    self.tt("dve", pa[0][0], TC, W[0][0], ALU.mult, [dTC, W[0][1]], [pa[0][1]])
                    self.tt("dve", pa[1][0], TS, W[1][0], ALU.mult, [dTS, W[1][1]], [pa[1][1]])
                    self.tt("pool", pa[2][0], TS, W[0][0], ALU.mult, [dTS, W[0][1]], [pa[2][1]])
                    self.tt("pool", pa[3][0], TC, W[1][0], ALU.mult, [dTC, W[1][1]], [pa[3][1]])
                    self.tt("dve", XR, pa[0][0], pa[1][0], ALU.subtract, [pa[0][1], pa[1][1]], [dX[0]])
                    self.tt("pool", XI, pa[2][0], pa[3][0], ALU.add, [pa[2][1], pa[3][1]], [dX[1]])
                    for ri in range(2):
                        self.cp("pool", carry[:, ri:ri + 1], W[ri][0][:, Q - 1:Q], [W[ri][1]], [dcarry])
                    bk, dbk = self.bank[4 + tb], self.dbank[4 + tb]
                    self.mm(bk[rows, :], CE[:, j, 0, :], XR, True, False, DP + [dX[0]], [dbk], tile_position=(0, 32 * s_))
                    self.mm(bk[rows, :], CE[:, j, 1, :], XI, False, True, DP + [dX[1]], [dbk], tile_position=(0, 32 * s_))
            for tb in range(NTB):
                sl = slice(tb * Q, (tb + 1) * Q)
                bk, dbk = self.bank[4 + tb], self.dbank[4 + tb]
                self.stt("dve", yp, uT[:, c, sl], dvec[:, c:c + 1], bk, ALU.mult, ALU.add, DP + [du[c][tb], dbk], [dyp])
                self.tt("pool", g1, yp, yp, ALU.mult, [dyp], [dg1])
                self.ts("pool", g1, g1, 0.044715, 1.0, ALU.mult, ALU.add, [dg1], [dg1])
                self.tt("pool", g1, g1, yp, ALU.mult, [dg1, dyp], [dg1])
                self.act(g2, g1, AF.Sigmoid, [dg1], [dg2], scale=2.0 * math.sqrt(2.0 / math.pi))
                self.tt("pool", ygT[:, c, sl], yp, g2, ALU.mult, [dyp, dg2], [dyg[c][tb]])
        wg = [A.alloc([4, 128], BF16), A.alloc([4, 128], BF16)]
        dwg = [Dep(), Dep()]
        for f in range(4):
            b = f % 2
            self.ldw(wg[b], I["w_glu_b"][0][:, f * 128:(f + 1) * 128].rearrange("(k p) f -> p k f", p=128), dwg[b])
            for tb in range(NTB):
                sl = slice(tb * Q, (tb + 1) * Q)
                bid = (f * NTB + tb) % 4
                bk, dbk = self.bank[bid], self.dbank[bid]
                self.proj_tile(wg[b], dwg[b], ygT, dyg, 4, tb, bk, dbk)
                self.act(g2, bk, AF.Sigmoid, [dbk] + DP, [dg2], bias=bglu[:, f:f + 1])
                self.tt("dve", ybT[:, f, sl], ygT[:, f, sl], g2, ALU.mult, [dyg[f][tb], dg2], [dyb[f][tb]])
        P.barrier()
        A.release()


    def deltanet(self, yaT, dya):
        A, I, P = self.A, self.I, self.P
        A.mark()
        xT, dxT, xnT, dxn = self.xT, self.dxT, self.xnT, self.dxn
        win = I["w_in_ab"][0]
        dsm = Dep()
        DS = [dsm]
        C = 128
        NCH = T // C
        stop = int(os.environ.get("DN_STOP", "9"))
        if stop < 9:
            for c_ in range(4):
                self.memset("pool", yaT[:, c_, :], 0.0, dya[c_])

        def small(shape=(C,)):
            return A.alloc(list(shape))
        wba = A.alloc([8, 8], BF16)
        dwba = Dep()
        self.ldw(wba, win[:, 2048:2056].rearrange("(k p) f -> p k f", p=128), dwba)
        bk, dbk = self.bank[0], self.dbank[0]
        for n in range(NCH):
            for k in range(8):
                self.mm(bk[:, n * 8:(n + 1) * 8], xnT[:, k, n * C:(n + 1) * C], wba[:, k, :], k == 0, k == 7,
                        [dwba, dxn[k][n // 4]], [dbk])
        BA = small((NCH, 8))
        self.cp("dve", BA, bk[:, 0:NCH * 8].rearrange("p (n e) -> p n e", e=8), [dbk], DS)
        alog = small((4,))
        dtb = small((4,))
        ong = small((1,))
        self.dma(alog, I["a_log_a"][0].partition_broadcast(128), [], DS, slow=True)
        self.dma(dtb, I["dt_bias_a"][0].partition_broadcast(128), [], DS, slow=True)
        self.dma(ong, I["onorm_g_a"][0].rearrange("(p o) -> p o", o=1), [], DS, slow=True)
        beta = small((NCH, 4))
        g_t = small((NCH, 4))
        gc_t = small((NCH, 4))
        gl_t = small((NCH, 4))
        bexp = small((NCH, 4))
        kdsc = small((NCH, 4))
        egl = small((NCH, 4))
        tm = small((NCH, 4))

        def bc4(a):
            return a.unsqueeze(1).to_broadcast([128, NCH, 4])
        self.act(beta, BA[:, :, 0:4], AF.Sigmoid, DS, DS)
        self.tt("dve", tm, BA[:, :, 4:8], bc4(dtb), ALU.add, DS, DS)
        self.act(tm, tm, AF.Exp, DS, DS)
        self.act(tm, tm, AF.Ln, DS, DS, bias=1.0)
        self.act(alog, alog, AF.Exp, DS, DS)
        self.tt("dve", g_t, tm, bc4(alog), ALU.mult, DS, DS)
        self.ts("dve", g_t, g_t, -1.0, None, ALU.mult, None, DS, DS)
        g2d = g_t.rearrange("p n h -> p (n h)")
        bk, dbk = self.bank[1], self.dbank[1]
        self.mm(bk[:, 0:64], self.tri, g2d, True, True, DS + [self.dconst], [dbk])
        self.cp("dve", gc_t.rearrange("p n h -> p (n h)"), bk[:, 0:64], [dbk], DS)
        self.mm(bk[:, 64:128], self.ones_f, g2d, True, True, DS + [self.dconst], [dbk])
        self.cp("dve", gl_t.rearrange("p n h -> p (n h)"), bk[:, 64:128], [dbk], DS)
        self.act(tm, gc_t, AF.Exp, DS, DS)
        self.tt("dve", bexp, beta, tm, ALU.mult, DS, DS)
        self.tt("dve", tm, gl_t, gc_t, ALU.subtract, DS, DS)
        self.act(kdsc, tm, AF.Exp, DS, DS)
        self.act(egl, gl_t, AF.Exp, DS, DS)
        cwq = small((4, 12))
        for i in range(4):
            self.dma(cwq[:, i, :], I["conv_qkv_a"][0, i].rearrange("(c p) -> p c", p=128), [], DS, slow=True)
        qkv = [A.alloc([T]) for _ in range(3)]
        dqkv = [Dep() for _ in range(3)]
        hb = A.alloc([3 + T])
        dhb = Dep()
        self.memset("pool", hb[:, 0:3], 0.0, [dhb])
        gsT = A.alloc([T], BF16)
        dgs = Dep()
        wt0 = A.alloc([8, 128], BF16)
        dwt0 = Dep()
        wt = [wt0, wt0]
        dwt = [dwt0, dwt0]
        rt, drt = A.alloc([512]), Dep()
        nm = ["trg", "e1", "dtm", "dnm", "egr", "qkt", "qgt", "rhsU", "rhsW", "kd", "u", "wT", "vnew", "on", "S",
              "junk", "Qa", "Qb", "Rf"]
        tls = [{n_: (small(), Dep()) for n_ in nm if n_ not in ("S", "junk", "on", "Rf", "vnew")} for _ in range(2)]
        tl = {"S": (small(), Dep())}
        vshared = (small(), Dep())
        for t_ in tls:
            t_["vnew"] = vshared
        PRs = [[(A.alloc([2, C]), Dep()), (A.alloc([2, C]), Dep())] for _ in range(2)]
        widx = 0
        nheads = int(os.environ.get("DN_HEADS", "4"))
        for c_ in range(nheads, 4):
            self.memset("pool", yaT[:, c_, :], 0.0, dya[c_])
        for h in range(nheads if stop >= 2 else 0):
            for qi in range(3):
                col = qi * 512 + h * 128
                ch = qi * 4 + h
                b = widx % 2
                widx += 1
                self.ldw(wt[b], win[:, col:col + 128].rearrange("(k p) f -> p k f", p=128), dwt[b])
                for tb in range(NTB):
                    bk, dbk = self.bank[4 + tb], self.dbank[4 + tb]
                    self.proj_tile(wt[b], dwt[b], xnT, dxn, 8, tb, bk, dbk)
                    self.cp("act" if tb % 2 else "dve", hb[:, 3 + tb * 512:3 + (tb + 1) * 512], bk, [dbk], [dhb])
                dst, dd = qkv[qi], dqkv[qi]
                self.ts("dve", dst, hb[:, 0:T], cwq[:, 0, ch:ch + 1], None, ALU.mult, None, [dhb] + DS, [dd])
                for tap in (1, 2, 3):
                    self.stt("dve", dst, hb[:, tap:tap + T], cwq[:, tap, ch:ch + 1], dst, ALU.mult, ALU.add,
                             [dhb, dd] + DS, [dd])
                self.act(dst, dst, AF.Silu, [dd], [dd])
                if qi < 2:
                    self.act(hb[:, 3:3 + T], dst, AF.Square, [dd], [dhb])
                    for tb in range(NTB):
                        sl = slice(tb * 512, (tb + 1) * 512)
                        bk, dbk = self.bank[tb % 2], self.dbank[tb % 2]
                        self.mm(bk, self.ones_f, hb[:, 3 + tb * 512:3 + (tb + 1) * 512], True, True,
                                [dhb, self.dconst], [dbk])
                        self.act(rt, bk, AF.Sqrt, [dbk], [drt], bias=1e-6)
                        P.emit("dve", lambda e, o=rt: e.reciprocal(out=o, in_=o), [drt], [drt])
                        if qi == 0:
                            self.stt("dve", dst[:, sl], dst[:, sl], 128.0 ** -0.5, rt, ALU.mult, ALU.mult,
                                     [dd, drt], [dd])
                        else:
                            self.tt("dve", dst[:, sl], dst[:, sl], rt, ALU.mult, [dd, drt], [dd])
            b = widx % 2
            widx += 1
            col = 1536 + h * 128
            self.ldw(wt[b], win[:, col:col + 128].rearrange("(k p) f -> p k f", p=128), dwt[b])
            for tb in range(NTB):
                bk, dbk = self.bank[4 + tb], self.dbank[4 + tb]
                self.proj_tile(wt[b], dwt[b], xnT, dxn, 8, tb, bk, dbk)
                self.act(gsT[:, tb * 512:(tb + 1) * 512], bk, AF.Silu, [dbk], [dgs])
            qT_, kT_, vT_ = qkv
            dq_, dk_, dv_ = dqkv
            S, dS_ = tl["S"]
            self.memset("pool", S, 0.0, [dS_])
            oh = hb[:, 3:3 + T]

            def mk(n):
                p = n % 2
                c_ = {"n": n, "cs": slice(n * C, (n + 1) * C), "t": tls[p]}
                c_["bA"], c_["dA"] = self.bank[p * 4 + 0], self.dbank[p * 4 + 0]
                c_["bB"] = [self.bank[p * 4 + 1], self.bank[p * 4 + 2]]
                c_["dB"] = [self.dbank[p * 4 + 1], self.dbank[p * 4 + 2]]
                c_["bC"], c_["dC"] = self.bank[p * 4 + 3], self.dbank[p * 4 + 3]
                c_["PR"] = list(PRs[p])
                c_["Q"] = [tls[p]["Qa"], tls[p]["Qb"]]
                return c_

            def dn_A(c_):
                n, cs, tl = c_["n"], c_["cs"], c_["t"]
                kc, qc = kT_[:, cs], qT_[:, cs]
                col = lambda a_: a_[:, n, h:h + 1]
                bA, dA = c_["bA"], c_["dA"]
                trg, dtrg = tl["trg"]
                self.ts("dve", trg, self.tri, col(g_t), None, ALU.mult, None, DS + [self.dconst], [dtrg])
                self.mm(bA[:, 0:C], self.ones_f, trg, True, True, [dtrg, self.dconst], [dA])
                e1, de1 = tl["e1"]
                e2, de2 = tl["trg"]
                dtm, ddtm = tl["dtm"]
                dnm, ddnm = tl["dnm"]
                egr, degr = tl["egr"]
                self.ts("dve", e1, bA[:, 0:C], col(gc_t), self.zcol[:, 0:1], ALU.subtract, ALU.min,
                        [dA, self.dconst] + DS, [de1])
                self.act(dtm, e1, AF.Exp, [de1], [ddtm])
                self.tt("dve", dtm, dtm, self.tri, ALU.mult, [ddtm, self.dconst], [ddtm])
                self.ts("dve", e2, bA[:, 0:C], col(gc_t), self.zcol[:, 0:1], ALU.subtract, ALU.max,
                        [dA, self.dconst] + DS, [de2])
                self.act(dnm, e2, AF.Exp, [de2], [ddnm], scale=-1.0)
                self.tt("dve", dnm, dnm, self.mlneg, ALU.mult, [ddnm, self.dconst], [ddnm])
                self.act(egr, bA[:, 0:C], AF.Exp, [dA], [degr])
                self.mm(bA[:, C:2 * C], kc, kc, True, True, [dk_], [dA])
                self.mm(bA[:, 2 * C:3 * C], kc, qc, True, True, [dk_, dq_], [dA])
                Qc, dQc = c_["Q"][0]
                self.stt("dve", Qc, bA[:, C:2 * C], col(beta), dnm, ALU.mult, ALU.mult, [dA, ddnm] + DS, [dQc])
                qkt, dqkt = tl["qkt"]
                self.tt("dve", qkt, bA[:, 2 * C:3 * C], dtm, ALU.mult, [dA, ddtm], [dqkt])
                qgt, dqgt = tl["qgt"]
                self.tt("dve", qgt, qc, egr, ALU.mult, [dq_, degr], [dqgt])
                self.tr(bA[:, 3 * C:4 * C], Qc, self.ident, [dQc, self.dconst], [dA])
                PRc, dPRc = c_["PR"][0]
                self.cp("act", PRc[:, 0, :], bA[:, 3 * C:4 * C], [dA], [dPRc])
                self.cp("dve", PRc[:, 1, :], self.ident, [self.dconst], [dPRc])

            def dn_N(c_, jj):
                (PRc, dPRc), (PRn, dPRn) = c_["PR"]
                (Qc, dQc), (Qn, dQn) = c_["Q"]
                bb, db = c_["bB"][jj % 2], c_["dB"][jj % 2]
                self.mm(bb[:, 0:2 * C], Qc, PRc.rearrange("p a b -> p (a b)"), True, True, [dQc, dPRc], [db])
                self.mm(bb[:, 2 * C:3 * C], PRc[:, 0, :], Qc, True, True, [dQc, dPRc], [db])
                self.cp("act", PRn[:, 0, :], bb[:, 0:C], [db], [dPRn])
                self.tt("dve", PRn[:, 1, :], PRc[:, 1, :], bb[:, C:2 * C], ALU.add, [dPRc, db], [dPRn])
                self.cp("act", Qn, bb[:, 2 * C:3 * C], [db], [dQn])
                c_["PR"] = [(PRn, dPRn), (PRc, dPRc)]
                c_["Q"] = [(Qn, dQn), (Qc, dQc)]

            def dn_F(c_):
                n, cs, tl = c_["n"], c_["cs"], c_["t"]
                kc, vc = kT_[:, cs], vT_[:, cs]
                col = lambda a_: a_[:, n, h:h + 1]
                (PRc, dPRc) = c_["PR"][0]
                (Qc, dQc) = c_["Q"][0]
                bb, db = c_["bB"][0], c_["dB"][0]
                bC, dC = c_["bC"], c_["dC"]
                Rf, dRf = tl["e1"]
                self.mm(bb[:, 0:C], Qc, PRc[:, 1, :], True, True, [dQc, dPRc], [db])
                self.tt("dve", Rf, PRc[:, 1, :], bb[:, 0:C], ALU.add, [dPRc, db], [dRf])
                rhsU, drU = tl["rhsU"]
                rhsW, drW = tl["rhsW"]
                kd, dkd = tl["kd"]
                self.tr(bC[:, 0:C], kc, self.ident, [dk_, self.dconst], [dC])
                self.tr(bC[:, C:2 * C], vc, self.ident, [dv_, self.dconst], [dC])
                self.ts("dve", rhsW, bC[:, 0:C], col(bexp), None, ALU.mult, None, [dC] + DS, [drW])
                self.ts("dve", kd, bC[:, 0:C], col(kdsc), None, ALU.mult, None, [dC] + DS, [dkd])
                self.ts("dve", rhsU, bC[:, C:2 * C], col(beta), None, ALU.mult, None, [dC] + DS, [drU])
                u_, du_ = tl["u"]
                wT_, dwT_ = tl["wT"]
                self.mm(bC[:, 2 * C:3 * C], Rf, rhsU, True, True, [dRf, drU], [dC])
                self.mm(bC[:, 3 * C:4 * C], rhsW, Rf, True, True, [dRf, drW], [dC])
                self.cp("act", u_, bC[:, 2 * C:3 * C], [dC], [du_])
                self.cp("act", wT_, bC[:, 3 * C:4 * C], [dC], [dwT_])

            def dn_S(c_):
                n, cs, tl = c_["n"], c_["cs"], c_["t"]
                col = lambda a_: a_[:, n, h:h + 1]
                u_, du_ = tl["u"]
                wT_, dwT_ = tl["wT"]
                qkt, dqkt = tl["qkt"]
                qgt, dqgt = tl["qgt"]
                kd, dkd = tl["kd"]
                vn, dvn = tl["vnew"]
                bb, db = c_["bB"][1], c_["dB"][1]
                self.mm(bb[:, 0:C], wT_, S, True, True, [dwT_, dS_], [db])
                self.tt("dve", vn, u_, bb[:, 0:C], ALU.subtract, [du_, db], [dvn])
                self.mm(bb[:, C:2 * C], S, qgt, True, True, [dqgt, dS_], [db])
                self.mm(bb[:, 3 * C:4 * C], vn, qkt, True, True, [dqkt, dvn], [db])
                self.mm(bb[:, 2 * C:3 * C], kd, vn, True, True, [dkd, dvn], [db])
                self.stt("dve", S, S, col(egl), bb[:, 2 * C:3 * C], ALU.mult, ALU.add, [dS_, db] + DS, [dS_])
                self.cp("dve", oh[:, cs], bb[:, C:2 * C], [db], [dhb])
                self.tt("dve", oh[:, cs], oh[:, cs], bb[:, 3 * C:4 * C], ALU.add, [dhb, db], [dhb])

            for n0 in range(0, NCH if stop >= 3 else 0, 2):
                cx = [mk(n0), mk(n0 + 1)]
                for c_ in cx:
                    dn_A(c_)
                for jj in range(6):
                    for c_ in cx:
                        dn_N(c_, jj)
                for c_ in cx:
                    dn_F(c_)
                for c_ in cx:
                    dn_S(c_)
            if stop < 7:
                continue
            oh = hb[:, 3:3 + T]
            sqb, dsqb = qT_, dq_
            self.act(sqb, oh, AF.Square, [dhb], [dsqb])
            for tb in range(NTB):
                sl = slice(tb * 512, (tb + 1) * 512)
                bk, dbk = self.bank[tb % 2], self.dbank[tb % 2]
                self.mm(bk, self.ones_f, sqb[:, sl], True, True, [dsqb, self.dconst], [dbk])
                self.act(rt, bk, AF.Sqrt, [dbk], [drt], scale=1.0 / 128.0, bias=EPS)
                P.emit("dve", lambda e, o=rt: e.reciprocal(out=o, in_=o), [drt], [drt])
                self.ts("dve", oh[:, sl], oh[:, sl], ong[:, 0:1], None, ALU.mult, None, [dhb] + DS, [dhb])
                self.tt("dve", oh[:, sl], oh[:, sl], rt, ALU.mult, [dhb, drt], [dhb])
                self.tt("pool", yaT[:, h, sl], oh[:, sl], gsT[:, sl], ALU.mult, [dhb, dgs], [dya[h][tb]])
        P.barrier()
        A.release()

    def hybrid(self, layer, sub=("s5", "dn")):
        A, I, P = self.A, self.I, self.P
        A.mark()
        xT, dxT, xnT, dxn = self.xT, self.dxT, self.xnT, self.dxn
        ybT = A.alloc([4, T], BF16)
        dyb = [[Dep() for _ in range(NTB)] for _ in range(4)]
        yaT = A.alloc([4, T], BF16)
        dya = [[Dep() for _ in range(NTB)] for _ in range(4)]
        self.rmsnorm(xT, dxT, 0, xnT, dxn, NTB, 512)
        if "s5" in sub:
            self.s5(ybT, dyb)
            self.rmsnorm(xT, dxT, 0, xnT, dxn, NTB, 512)
        else:
            for c in range(4):
                self.memset("pool", ybT[:, c, :], 0.0, dyb[c])
        if "dn" in sub:
            self.deltanet(yaT, dya)
        else:
            for c in range(4):
                self.memset("pool", yaT[:, c, :], 0.0, dya[c])
        wo = I["w_out_ab"][0]
        wt = [A.alloc([8, 128], BF16), A.alloc([8, 128], BF16)]
        dwt = [Dep(), Dep()]
        for f in range(8):
            b = f % 2
            self.ldw(wt[b], wo[:, f * 128:(f + 1) * 128].rearrange("(k p) f -> p k f", p=128), dwt[b])
            for tb in range(NTB):
                bid = (f * NTB + tb) % 2
                bk, dbk = self.bank[bid], self.dbank[bid]
                sl = slice(tb * 512, (tb + 1) * 512)
                for k in range(8):
                    src, dsrc = (yaT, dya) if k < 4 else (ybT, dyb)
                    self.mm(bk, wt[b][:, k, :], src[:, k % 4, sl], k == 0, k == 7, [dwt[b], dsrc[k % 4][tb]], [dbk])
                self.tt("dve", xT[:, f, sl], bk, xT[:, f, sl], ALU.add, [dbk, dxT[f][tb]], [dxT[f][tb]])
        P.barrier()
        A.release()


def build_nc(layers=(0, 1), parts=("mix", "xa", "ffn"), dbg=None, sub=("s5", "dn")):
    nc = bass.Bass("TRN2", target_bir_lowering=False)
    st = ExitStack()
    kb = KB(nc, st, dbg)
    kb.build(layers, parts, sub)
    kb.P.run(st)
    st.close()
    return nc, kb


def kernel(**inputs):
    nc, _ = build_nc()
    names = [n for n, _ in INPUT_SHAPES]
    shared = {n: np.ascontiguousarray(np.asarray(inputs[n], dtype=np.float32)) for n in names if n not in ("x", "mem")}
    x = np.asarray(inputs["x"], dtype=np.float32)
    mem = np.asarray(inputs["mem"], dtype=np.float32)
    in_maps = []
    for b in range(8):
        m = dict(shared)
        m["x"] = np.ascontiguousarray(x[b])
        m["mem"] = np.ascontiguousarray(mem[b])
        in_maps.append(m)
    res = run_bass_kernel_spmd(nc, in_maps, core_ids=list(range(8)))
    return np.stack([np.asarray(r["out"], dtype=np.float32) for r in res.results], axis=0)
```

```python
import math
import os
from contextlib import ExitStack
import numpy as np
import concourse.bass as bass
import concourse.mybir as mybir
from concourse.bass_utils import run_bass_kernel_spmd

F32 = mybir.dt.float32
BF16 = mybir.dt.bfloat16
I32 = mybir.dt.int32
AF = mybir.ActivationFunctionType
ALU = mybir.AluOpType

COMPUTE = ("pe", "act", "dve", "pool")
NDMASEM = 24
ARENA_WORDS = 53000

T = 2048
D = 1024
NTB = 4
DFF = 2816
NF = 22
MEM = 256
EPS = 1e-6
MAGIC = 12582912.0
TWO_PI = 2.0 * math.pi


class Dep:
    __slots__ = ("w", "r", "rd")

    def __init__(self):
        self.w = None
        self.r = {}
        self.rd = []


class Op:
    __slots__ = ("eng", "fn", "deps", "flag", "idx", "val", "snap", "dma", "dsem", "dval", "waits")

    def __init__(self, eng, fn, dma):
        self.eng = eng
        self.fn = fn
        self.dma = dma
        self.deps = []
        self.flag = False
        self.val = 0
        self.snap = None
        self.dsem = -1
        self.dval = 0
        self.waits = []


class Prog:
    def __init__(self, nc):
        self.nc = nc
        self.ops = {e: [] for e in COMPUTE + ("sp",)}
        self.all = []
        self.dma_count = 0
        self.dma_count2 = 0
        self.dma_last = [None] * NDMASEM
        self.dma_vals = [0] * NDMASEM
        self.pending = {e: [] for e in COMPUTE + ("sp",)}
        self.dmas_since_barrier = []

    def emit(self, eng, fn, reads=(), writes=(), dma=False):
        op = Op(eng, fn, dma)
        deps = {}
        for d in reads:
            if d.w is not None:
                deps[id(d.w)] = d.w
        for d in writes:
            if d.w is not None:
                deps[id(d.w)] = d.w
            for o in d.r.values():
                deps[id(o)] = o
            for o in d.rd:
                deps[id(o)] = o
        for o in self.pending[eng]:
            deps[id(o)] = o
        self.pending[eng] = []
        op.deps = list(deps.values())
        for d in reads:
            if dma:
                d.rd.append(op)
            else:
                d.r[eng] = op
        for d in writes:
            d.w = op
            d.r = {}
            d.rd = []
        op.idx = len(self.ops[eng])
        self.ops[eng].append(op)
        self.all.append(op)
        if dma:
            half = NDMASEM // 2
            if eng == "sp":
                s = self.dma_count % half
                self.dma_count += 1
            else:
                s = half + self.dma_count2 % half
                self.dma_count2 += 1
            prev = self.dma_last[s]
            if prev is not None:
                op.deps.append(prev)
            self.dma_vals[s] += 16
            op.dsem = s
            op.dval = self.dma_vals[s]
            self.dma_last[s] = op
            self.dmas_since_barrier.append(op)
        return op

    def barrier(self):
        lasts = []
        for e in self.ops:
            for o in reversed(self.ops[e]):
                if not o.dma:
                    lasts.append(o)
                    break
        lasts += list(self.dmas_since_barrier)
        self.dmas_since_barrier = []
        for e in self.pending:
            self.pending[e] = list(lasts)

    def resolve(self):
        seen = {e: {} for e in self.ops}
        for op in self.all:
            E = op.eng
            s = seen[E]
            for D_ in op.deps:
                if D_.dma:
                    key = ("d", D_.dsem)
                    v = D_.dval
                else:
                    key = D_.eng
                    v = D_.idx + 1
                    if D_.eng == "pe" and E == "pe" and not op.dma:
                        continue
                if s.get(key, 0) >= v:
                    continue
                op.waits.append(D_)
                if not D_.dma:
                    D_.flag = True
                s[key] = v
                if D_.snap is not None:
                    for k, vv in D_.snap.items():
                        if s.get(k, 0) < vv:
                            s[k] = vv
            op.snap = dict(s)
        for e in COMPUTE:
            c = 0
            for op in self.ops[e]:
                if op.dma:
                    continue
                if op.flag:
                    c += 1
                    op.val = c

    def run(self, stack):
        nc = self.nc
        self.resolve()
        sems = {e: stack.enter_context(nc.semaphore("s_" + e)) for e in COMPUTE}
        dsems = [stack.enter_context(nc.semaphore("d%d" % i)) for i in range(NDMASEM)]
        block = stack.enter_context(nc.Block())

        def body(ename):
            def f(e):
                for op in self.ops[ename]:
                    for D_ in op.waits:
                        if D_.dma:
                            e.wait_ge(dsems[D_.dsem], D_.dval)
                        else:
                            e.wait_ge(sems[D_.eng], D_.val)
                    ins = op.fn(e)
                    if op.dma:
                        ins.then_inc(dsems[op.dsem], 16)
                    elif op.flag:
                        ins.then_inc(sems[ename], 1)
            return f

        block.tensor(body("pe"))
        block.scalar(body("act"))
        block.vector(body("dve"))
        block.gpsimd(body("pool"))
        block.sync(body("sp"))


class Arena:
    def __init__(self, ap2d, nwords):
        self.ap = ap2d
        self.n = nwords
        self.off = 0
        self.marks = []

    def alloc(self, shape, dtype=F32, parts=128):
        n = int(np.prod(shape))
        words = n if dtype in (F32, I32) else (n + 1) // 2
        assert self.off + words <= self.n, ("arena overflow", self.off, words, self.n)
        a = self.ap[0:parts, self.off:self.off + words]
        self.off += words
        if dtype != F32:
            a = a.bitcast(dtype)
            if a.shape[-1] != n:
                a = a[:, 0:n]
        if len(shape) > 1:
            names = " ".join("d%d" % i for i in range(len(shape)))
            kw = {"d%d" % i: int(shape[i]) for i in range(len(shape))}
            a = a.rearrange("p (%s) -> p %s" % (names, names), **kw)
        return a

    def mark(self):
        self.marks.append(self.off)

    def release(self):
        self.off = self.marks.pop()


INPUT_SHAPES = [
    ("x", [T, D]), ("mem", [MEM, D]),
    ("norm_mix_g", [2, D]), ("norm_xa_g", [2, D]), ("norm_ffn_g", [2, D]),
    ("norm_mem_g", [D]), ("norm_final_g", [D]),
    ("w_in_ab", [1, D, 2568]), ("conv_qkv_a", [1, 4, 1536]), ("a_log_a", [1, 4]), ("dt_bias_a", [1, 4]),
    ("onorm_g_a", [1, 128]),
    ("ssm_lambda_re", [1, 32, 64]), ("ssm_lambda_im", [1, 32, 64]),
    ("ssm_b_re", [1, 32, 64, 16]), ("ssm_b_im", [1, 32, 64, 16]),
    ("ssm_c_re", [1, 32, 16, 64]), ("ssm_c_im", [1, 32, 16, 64]),
    ("ssm_d", [1, 32, 16]), ("ssm_log_dt", [1, 32]),
    ("w_glu_b", [1, 512, 512]), ("b_glu_b", [1, 512]), ("w_out_ab", [1, D, D]),
    ("pool_w", [1, 4, 256, 256]), ("pool_scale", [1, D]),
    ("xa_wq", [2, D, D]), ("xa_wkv", [2, D, 2 * D]), ("xa_wo", [2, D, D]),
    ("ffn_w_up", [2, D, 2 * DFF]), ("ffn_conv", [2, 3, 2 * DFF]), ("ffn_w_down", [2, DFF, D]),
]


class KB:
    def __init__(self, nc, st, dbg=None):
        self.nc = nc
        self.dbg = dbg or {}
        self.I = {n: nc.dram_tensor(n, s, F32, kind="ExternalInput").ap() for n, s in INPUT_SHAPES}
        self.out = nc.dram_tensor("out", [T, D], F32, kind="ExternalOutput").ap()
        self.dbg_out = {}
        for n, s in self.dbg.items():
            self.dbg_out[n] = nc.dram_tensor("dbg_" + n, s, F32, kind="ExternalOutput").ap()
        sb = st.enter_context(nc.sbuf_tensor("arena", [128, ARENA_WORDS], F32))
        ps = st.enter_context(nc.psum_tensor("ps", [128, 4096], F32))
        self.A = Arena(sb, ARENA_WORDS)
        self.P = Prog(nc)
        self.ps = ps
        self.bank = [ps[:, b * 512:(b + 1) * 512] for b in range(8)]
        self.dbank = [Dep() for _ in range(8)]
        self.alt = 0

    def mm(self, out, lhsT, rhs, start, stop, reads, writes, **kw):
        self.P.emit("pe", lambda e: e.matmul(out, lhsT=lhsT, rhs=rhs, start=start, stop=stop, **kw), reads, writes)

    def tr(self, out, in_, ident, reads, writes):
        self.P.emit("pe", lambda e: e.transpose(out, in_, ident), reads, writes)

    def act(self, out, in_, func, reads, writes, bias=None, scale=1.0, accum_out=None):
        kw = {}
        if bias is not None:
            kw["bias"] = bias
        if accum_out is not None:
            kw["accum_out"] = accum_out
        self.P.emit("act", lambda e: e.activation(out=out, in_=in_, func=func, scale=scale, **kw), reads, writes)

    def tt(self, eng, out, in0, in1, op, reads, writes):
        self.P.emit(eng, lambda e: e.tensor_tensor(out=out, in0=in0, in1=in1, op=op), reads, writes)

    def ts(self, eng, out, in0, s1, s2, op0, op1, reads, writes):
        if op1 is None:
            self.P.emit(eng, lambda e: e.tensor_scalar(out=out, in0=in0, scalar1=s1, scalar2=None, op0=op0), reads, writes)
        else:
            self.P.emit(eng, lambda e: e.tensor_scalar(out=out, in0=in0, scalar1=s1, scalar2=s2, op0=op0, op1=op1),
                        reads, writes)

    def stt(self, eng, out, in0, scalar, in1, op0, op1, reads, writes):
        eng = "dve"
        self.P.emit(eng, lambda e: e.scalar_tensor_tensor(out=out, in0=in0, scalar=scalar, in1=in1, op0=op0, op1=op1),
                    reads, writes)

    def cp(self, eng, out, in_, reads, writes):
        if eng == "act":
            self.P.emit("act", lambda e: e.activation(out=out, in_=in_, func=AF.Copy), reads, writes)
        else:
            self.P.emit(eng, lambda e: e.tensor_copy(out=out, in_=in_), reads, writes)

    def memset(self, eng, ap, val, writes):
        self.P.emit(eng, lambda e: e.memset(ap, val), (), writes)

    def dma(self, out, in_, reads, writes, q="sp", slow=False):
        if slow:
            self.P.emit(q, lambda e: e.dma_start(out=out, in_=in_, allow_slow_non_contiguous=True), reads, writes, dma=True)
        else:
            self.P.emit(q, lambda e: e.dma_start(out=out, in_=in_), reads, writes, dma=True)

    def ldw(self, out, in_, dep):
        self.P.emit("pool", lambda e: e.dma_start(out=out, in_=in_), (), [dep], dma=True)

    def ve(self):
        return "dve"

    def dump(self, name, ap, deps):
        if name in self.dbg_out:
            self.dma(self.dbg_out[name], ap, deps, [])

    def consts(self):
        A, P = self.A, self.P
        self.dconst = Dep()
        dc = [self.dconst]
        self.ones_f = A.alloc([128])
        self.negones = A.alloc([128])
        self.ones_bf = A.alloc([128], BF16)
        self.ident = A.alloc([128])
        self.tri = A.alloc([128])
        self.ml = A.alloc([128])
        self.mlneg = A.alloc([128])
        self.memset("pool", self.ones_f, 1.0, dc)
        self.zcol = A.alloc([1])
        self.memset("pool", self.zcol, 0.0, dc)
        self.memset("pool", self.negones, -1.0, dc)
        self.memset("pool", self.ones_bf, 1.0, dc)
        P.emit("pool", lambda e: e.affine_select(out=self.ident, in_=self.ones_f, pattern=[[1, 128]],
                                                 compare_op=ALU.is_equal, fill=0.0, base=0, channel_multiplier=-1),
               dc, dc)
        P.emit("pool", lambda e: e.affine_select(out=self.tri, in_=self.ones_f, pattern=[[1, 128]],
                                                 compare_op=ALU.is_ge, fill=0.0, base=0, channel_multiplier=-1),
               dc, dc)
        P.emit("pool", lambda e: e.affine_select(out=self.ml, in_=self.ones_f, pattern=[[-1, 128]],
                                                 compare_op=ALU.is_ge, fill=0.0, base=0, channel_multiplier=1),
               dc, dc)
        P.emit("pool", lambda e: e.affine_select(out=self.mlneg, in_=self.negones, pattern=[[-1, 128]],
                                                 compare_op=ALU.is_gt, fill=0.0, base=0, channel_multiplier=1),
               dc, dc)
        self.gT = A.alloc([8, 8])
        I = self.I
        srcs = [I["norm_mix_g"][0], I["norm_mix_g"][1], I["norm_xa_g"][0], I["norm_xa_g"][1],
                I["norm_ffn_g"][0], I["norm_ffn_g"][1], I["norm_final_g"], I["norm_mem_g"]]
        for i, s in enumerate(srcs):
            self.dma(self.gT[:, i, :], s.rearrange("(c p) -> p c", p=128), [], dc, slow=True)
        self.sq = A.alloc([8, 512], BF16)
        self.dsq = Dep()
        self.rt = A.alloc([512])
        self.drt = Dep()
        self.rs = A.alloc([512])
        self.drs = Dep()

    def rmsnorm(self, src, dsrc, gi, dst, ddst, nblk, bs, bankid=0):
        for tb in range(nblk):
            sl = slice(tb * bs, (tb + 1) * bs)
            rd = [dsrc[c][tb] for c in range(8)]
            self.act(self.sq[:, :, 0:bs], src[:, :, sl], AF.Square, rd, [self.dsq])
            bk, dbk = self.bank[bankid], self.dbank[bankid]
            for c in range(8):
                self.mm(bk[:, 0:bs], self.ones_bf, self.sq[:, c, 0:bs], c == 0, c == 7,
                        [self.dsq, self.dconst], [dbk])
            self.act(self.rt[:, 0:bs], bk[:, 0:bs], AF.Ln, [dbk], [self.drt], bias=self.epsb, scale=1.0 / D)
            self.act(self.rs[:, 0:bs], self.rt[:, 0:bs], AF.Exp, [self.drt], [self.drs], scale=-0.5)
            for c in range(8):
                self.stt(self.ve(), dst[:, c, sl], src[:, c, sl], self.gT[:, gi, c:c + 1], self.rs[:, 0:bs],
                         ALU.mult, ALU.mult, [dsrc[c][tb], self.drs, self.dconst], [ddst[c][tb]])

    def load_T(self, src_dram, ntile, dstT, ddst, tiles_per_blk):
        A = self.A
        A.mark()
        xin = [A.alloc([D]), A.alloc([D])]
        dxin = [Dep(), Dep()]
        for i in range(ntile):
            b = i % 2
            self.dma(xin[b], src_dram[i * 128:(i + 1) * 128, :], [], [dxin[b]])
            for half in range(2):
                bid = (i % 2) * 2 + half
                bk, dbk = self.bank[bid], self.dbank[bid]
                for cc in range(4):
                    c = half * 4 + cc
                    self.tr(bk[:, cc * 128:(cc + 1) * 128], xin[b][:, c * 128:(c + 1) * 128], self.ident,
                            [dxin[b], self.dconst], [dbk])
                eng = "act" if half == 0 else "dve"
                self.cp(eng, dstT[:, half * 4:half * 4 + 4, i * 128:(i + 1) * 128],
                        bk.rearrange("p (c t) -> p c t", c=4), [dbk],
                        [ddst[c][i // tiles_per_blk] for c in range(half * 4, half * 4 + 4)])
        self.P.barrier()
        A.release()

    def proj_tile(self, wt, dwt, srcT, dsrc, nk, tb, bk, dbk, bs=512, koff=0):
        sl = slice(tb * bs, (tb + 1) * bs)
        for k in range(nk):
            self.mm(bk[:, 0:bs], wt[:, k, :], srcT[:, koff + k, sl], k == 0, k == nk - 1,
                    [dwt, dsrc[koff + k][tb]], [dbk])

    def xattn(self, layer):
        A, I = self.A, self.I
        A.mark()
        xT, dxT, xnT, dxn = self.xT, self.dxT, self.xnT, self.dxn
        self.rmsnorm(xT, dxT, 2 + layer, xnT, dxn, NTB, 512)
        memT = A.alloc([8, MEM])
        dmemT = [[Dep()] for _ in range(8)]
        self.load_T(I["mem"], 2, memT, dmemT, 2)
        mnT = A.alloc([8, MEM], BF16)
        dmn = [[Dep()] for _ in range(8)]
        self.rmsnorm(memT, dmemT, 7, mnT, dmn, 1, MEM)
        wkv = I["xa_wkv"][layer]
        kT = A.alloc([8, MEM], BF16)
        dkT = [Dep() for _ in range(8)]
        wt = [A.alloc([8, 128], BF16), A.alloc([8, 128], BF16)]
        dwt = [Dep(), Dep()]
        for f in range(8):
            b = f % 2
            self.ldw(wt[b], wkv[:, f * 128:(f + 1) * 128].rearrange("(k p) f -> p k f", p=128), dwt[b])
            bk, dbk = self.bank[4 + b], self.dbank[4 + b]
            for k in range(8):
                self.mm(bk[:, 0:MEM], wt[b][:, k, :], mnT[:, k, :], k == 0, k == 7, [dwt[b], dmn[k][0]], [dbk])
            self.cp("act" if b else "dve", kT[:, f, :], bk[:, 0:MEM], [dbk], [dkT[f]])
        vv = A.alloc([2, D], BF16)
        dvv = [[Dep(), Dep()] for _ in range(2)]
        wv = [A.alloc([8, 512], BF16), A.alloc([8, 512], BF16)]
        dwv = [Dep(), Dep()]
        for n in range(2):
            self.ldw(wv[n], wkv[:, D + n * 512:D + (n + 1) * 512].rearrange("(k p) f -> p k f", p=128), dwv[n])
            for mc in range(2):
                bk, dbk = self.bank[6 + mc], self.dbank[6 + mc]
                for k in range(8):
                    self.mm(bk, mnT[:, k, mc * 128:(mc + 1) * 128], wv[n][:, k, :], k == 0, k == 7,
                            [dwv[n], dmn[k][0]], [dbk])
                self.cp("act" if mc else "dve", vv[:, mc, n * 512:(n + 1) * 512], bk, [dbk], [dvv[mc][n]])
        oT = A.alloc([8, T], BF16)
        doT = [[Dep() for _ in range(NTB)] for _ in range(8)]
        qT = A.alloc([2, T], BF16)
        dq = [[Dep() for _ in range(NTB)] for _ in range(2)]
        pT = A.alloc([2, 512], BF16)
        dpT = [Dep(), Dep()]
        rinv = A.alloc([512])
        drinv = Dep()
        wq = I["xa_wq"][layer]
        for h in range(4):
            for dc in range(2):
                f = 2 * h + dc
                b = f % 2
                self.ldw(wt[b], wq[:, f * 128:(f + 1) * 128].rearrange("(k p) f -> p k f", p=128), dwt[b])
                for tb in range(NTB):
                    bid = (f * NTB + tb) % 2
                    bk, dbk = self.bank[bid], self.dbank[bid]
                    self.proj_tile(wt[b], dwt[b], xnT, dxn, 8, tb, bk, dbk)
                    self.cp("act" if tb % 2 else "dve", qT[:, dc, tb * 512:(tb + 1) * 512], bk, [dbk], [dq[dc][tb]])
            for tb in range(NTB):
                sl = slice(tb * 512, (tb + 1) * 512)
                for mc in range(2):
                    bk, dbk = self.bank[2 + mc], self.dbank[2 + mc]
                    for dc in range(2):
                        self.mm(bk, kT[:, 2 * h + dc, mc * 128:(mc + 1) * 128], qT[:, dc, sl], dc == 0, dc == 1,
                                [dkT[2 * h + dc], dq[dc][tb]], [dbk])
                    self.act(pT[:, mc, :], bk, AF.Exp, [dbk], [dpT[mc]], scale=1.0 / 16.0)
                bs_, dbs = self.bank[4], self.dbank[4]
                for mc in range(2):
                    self.mm(bs_, self.ones_bf, pT[:, mc, :], mc == 0, mc == 1, [dpT[mc], self.dconst], [dbs])
                self.act(rinv, bs_, AF.Ln, [dbs], [drinv])
                self.act(rinv, rinv, AF.Exp, [drinv], [drinv], scale=-1.0)
                for dvc in range(2):
                    bk, dbk = self.bank[5 + dvc], self.dbank[5 + dvc]
                    col = h * 256 + dvc * 128
                    for mc in range(2):
                        self.mm(bk, vv[:, mc, col:col + 128], pT[:, mc, :], mc == 0, mc == 1,
                                [dvv[mc][col // 512], dpT[mc]], [dbk])
                    self.tt("dve", oT[:, 2 * h + dvc, sl], bk, rinv, ALU.mult, [dbk, drinv], [doT[2 * h + dvc][tb]])
        wo = I["xa_wo"][layer]
        for f in range(8):
            b = f % 2
            self.ldw(wt[b], wo[:, f * 128:(f + 1) * 128].rearrange("(k p) f -> p k f", p=128), dwt[b])
            for tb in range(NTB):
                bid = (f * NTB + tb) % 2
                bk, dbk = self.bank[bid], self.dbank[bid]
                self.proj_tile(wt[b], dwt[b], oT, doT, 8, tb, bk, dbk)
                sl = slice(tb * 512, (tb + 1) * 512)
                self.tt("dve", xT[:, f, sl], bk, xT[:, f, sl], ALU.add, [dbk, dxT[f][tb]], [dxT[f][tb]])
        self.P.barrier()
        A.release()

    def ffn(self, layer):
        A, I = self.A, self.I
        A.mark()
        xT, dxT, xnT, dxn = self.xT, self.dxT, self.xnT, self.dxn
        self.rmsnorm(xT, dxT, 4 + layer, xnT, dxn, NTB, 512)
        wup = I["ffn_w_up"][layer]
        wdn = I["ffn_w_down"][layer]
        cw = A.alloc([3, 2 * NF])
        dcw = Dep()
        for i in range(3):
            self.dma(cw[:, i, :], I["ffn_conv"][layer, i].rearrange("(c p) -> p c", p=128), [], [dcw], slow=True)
        actT = A.alloc([NF, 1024], BF16)
        dact = [[Dep(), Dep()] for _ in range(NF)]
        halo = A.alloc([2 * NF, 2])
        dhalo = [Dep() for _ in range(2 * NF)]
        hb = [A.alloc([1026]) for _ in range(4)]
        dhb = [Dep() for _ in range(4)]
        acc = [A.alloc([1024]) for _ in range(2)]
        dacc = [Dep() for _ in range(2)]
        ptmp = A.alloc([1024])
        dptmp = Dep()
        wu = [A.alloc([2, 8, 128], BF16), A.alloc([2, 8, 128], BF16)]
        dwu = [Dep(), Dep()]
        wd = [A.alloc([NF, 128], BF16), A.alloc([NF, 128], BF16)]
        dwd = [Dep(), Dep()]
        it = 0
        for th in range(2):
            for f in range(NF):
                b = f % 2
                for gv in range(2):
                    col = gv * DFF + f * 128
                    self.ldw(wu[b][:, gv], wup[:, col:col + 128].rearrange("(k p) f -> p k f", p=128), dwu[b])
                for gv in range(2):
                    ch = gv * NF + f
                    u = (it % 2) * 2 + gv
                    for t2 in range(2):
                        bid = gv * 4 + (it % 2) * 2 + t2
                        bk, dbk = self.bank[bid], self.dbank[bid]
                        self.proj_tile(wu[b][:, gv], dwu[b], xnT, dxn, 8, th * 2 + t2, bk, dbk)
                        self.cp("act", hb[u][:, 2 + t2 * 512:2 + (t2 + 1) * 512], bk, [dbk], [dhb[u]])
                    if th == 0:
                        self.memset("dve", hb[u][:, 0:2], 0.0, [dhb[u]])
                    else:
                        self.cp("dve", hb[u][:, 0:2], halo[:, ch, :], [dhalo[ch]], [dhb[u]])
                    if th == 0:
                        self.cp("dve", halo[:, ch, :], hb[u][:, 1024:1026], [dhb[u]], [dhalo[ch]])
                    self.ts("dve", acc[gv], hb[u][:, 0:1024], cw[:, 0, ch:ch + 1], None, ALU.mult, None,
                            [dhb[u], dcw], [dacc[gv]])
                    self.stt("dve", acc[gv], hb[u][:, 1:1025], cw[:, 1, ch:ch + 1], acc[gv], ALU.mult, ALU.add,
                             [dhb[u], dcw, dacc[gv]], [dacc[gv]])
                    self.stt("dve", acc[gv], hb[u][:, 2:1026], cw[:, 2, ch:ch + 1], acc[gv], ALU.mult, ALU.add,
                             [dhb[u], dcw, dacc[gv]], [dacc[gv]])
                self.act(acc[0], acc[0], AF.Silu, [dacc[0]], [dacc[0]])
                self.tt("dve", actT[:, f, :], acc[0], acc[1], ALU.mult, [dacc[0], dacc[1]],
                        [dact[f][0], dact[f][1]])
                it += 1
            for ft in range(8):
                b = ft % 2
                self.ldw(wd[b], wdn[:, ft * 128:(ft + 1) * 128].rearrange("(k p) f -> p k f", p=128), dwd[b])
                for t2 in range(2):
                    tb = th * 2 + t2
                    bid = (ft * 2 + t2) % 4
                    bk, dbk = self.bank[bid], self.dbank[bid]
                    sl2 = slice(t2 * 512, (t2 + 1) * 512)
                    for k in range(NF):
                        self.mm(bk, wd[b][:, k, :], actT[:, k, sl2], k == 0, k == NF - 1, [dwd[b], dact[k][t2]], [dbk])
                    sl = slice(tb * 512, (tb + 1) * 512)
                    self.tt("dve", xT[:, ft, sl], bk, xT[:, ft, sl], ALU.add, [dbk, dxT[ft][tb]], [dxT[ft][tb]])
        self.P.barrier()
        A.release()

    def poolmix(self, layer):
        A, I = self.A, self.I
        A.mark()
        xT, dxT, xnT, dxn = self.xT, self.dxT, self.xnT, self.dxn
        self.rmsnorm(xT, dxT, 0 + layer, xnT, dxn, NTB, 512)
        PAD = 16
        bufs = [A.alloc([PAD + T]) for _ in range(3)]
        dbuf = [Dep() for _ in range(3)]
        for b in range(3):
            self.memset("pool", bufs[b][:, 0:PAD], 0.0, [dbuf[b]])
        diffT = A.alloc([2, T], BF16)
        ddiff = [[Dep() for _ in range(NTB)] for _ in range(2)]
        pw = A.alloc([2, 256], BF16)
        dpw = Dep()
        psc = A.alloc([8])
        dpsc = Dep()
        self.dma(psc, I["pool_scale"][0].rearrange("(c p) -> p c", p=128), [], [dpsc], slow=True)
        cnt = A.alloc([16])
        dcnt = Dep()
        invc = A.alloc([4, 16])
        self.P.emit("pool", lambda e: e.iota(cnt, pattern=[[1, 16]], base=1, channel_multiplier=0,
                                             allow_small_or_imprecise_dtypes=True), [], [dcnt])
        for gi in range(4):
            w = 2 << gi
            self.ts("dve", invc[:, gi, :], cnt, float(w), None, ALU.min, None, [dcnt], [dcnt])
        self.P.emit("dve", lambda e: e.reciprocal(out=invc, in_=invc), [dcnt], [dcnt])
        for gi in range(4):
            w = 2 << gi
            self.ldw(pw, I["pool_w"][0, gi].rearrange("(k p) f -> p k f", p=128), dpw)
            for cc in range(2):
                c = 2 * gi + cc
                rd = [dxn[c][tb] for tb in range(NTB)]
                cur = 0
                self.cp("act", bufs[0][:, PAD:], xnT[:, c, :], rd, [dbuf[0]])
                s = 1
                src = 0
                while s < w:
                    nxt = 1 if src != 1 else 2
                    eng = self.ve()
                    self.tt(eng, bufs[nxt][:, PAD:], bufs[src][:, PAD:], bufs[src][:, PAD - s:PAD - s + T], ALU.add,
                            [dbuf[src]], [dbuf[nxt]])
                    src = nxt
                    s *= 2
                wr = [ddiff[cc][tb] for tb in range(NTB)]
                self.stt("dve", diffT[:, cc, :], bufs[src][:, PAD:], 1.0 / w, bufs[0][:, PAD:], ALU.mult, ALU.subtract,
                         [dbuf[src], dbuf[0]], wr)
                tmp = bufs[2 if src != 2 else 1]
                dtmp = dbuf[2 if src != 2 else 1]
                self.tt("dve", tmp[:, PAD:PAD + w], bufs[src][:, PAD:PAD + w], invc[:, gi, 0:w], ALU.mult,
                        [dbuf[src], dcnt], [dtmp])
                self.tt("dve", diffT[:, cc, 0:w], tmp[:, PAD:PAD + w], bufs[0][:, PAD:PAD + w], ALU.subtract,
                        [dtmp, dbuf[0]], [ddiff[cc][0]])
            for e_ in range(2):
                f = 2 * gi + e_
                for tb in range(NTB):
                    bid = (f * NTB + tb) % 4
                    bk, dbk = self.bank[bid], self.dbank[bid]
                    sl = slice(tb * 512, (tb + 1) * 512)
                    for cc in range(2):
                        self.mm(bk, pw[:, cc, e_ * 128:(e_ + 1) * 128], diffT[:, cc, sl], cc == 0, cc == 1,
                                [dpw, ddiff[cc][tb]], [dbk])
                    self.stt("dve", xT[:, f, sl], bk, psc[:, f:f + 1], xT[:, f, sl], ALU.mult, ALU.add,
                             [dbk, dpsc, dxT[f][tb]], [dxT[f][tb]])
        self.P.barrier()
        A.release()

    def final(self):
        A = self.A
        A.mark()
        xT, dxT = self.xT, self.dxT
        xf = A.alloc([8, 512])
        dxf = [[Dep()] for _ in range(8)]
        ot = [A.alloc([D]), A.alloc([D])]
        dot = [Dep(), Dep()]
        n = 0
        for tb in range(NTB):
            srcv = xT[:, :, tb * 512:(tb + 1) * 512]
            dsv = [[dxT[c][tb]] for c in range(8)]
            self.rmsnorm(srcv, dsv, 6, xf, dxf, 1, 512)
            for j in range(4):
                b = n % 2
                for half in range(2):
                    bid = 4 + b * 2 + half
                    bk, dbk = self.bank[bid], self.dbank[bid]
                    for cc in range(4):
                        c = half * 4 + cc
                        self.tr(bk[:, cc * 128:(cc + 1) * 128], xf[:, c, j * 128:(j + 1) * 128], self.ident,
                                [dxf[c][0], self.dconst], [dbk])
                    self.cp("act" if half else "dve", ot[b][:, half * 512:(half + 1) * 512], bk, [dbk], [dot[b]])
                i = tb * 4 + j
                self.dma(self.out[i * 128:(i + 1) * 128, :], ot[b], [dot[b]], [])
                n += 1
        A.release()

    def build(self, layers=(0, 1), parts=("mix", "xa", "ffn"), sub=("s5", "dn")):
        A = self.A
        self.sub = sub
        self.consts()
        self.epsb = EPS
        self.halfpi = math.pi / 2.0
        self.xT = A.alloc([8, T])
        self.dxT = [[Dep() for _ in range(NTB)] for _ in range(8)]
        self.xn_raw = A.alloc([8 * T // 2])
        self.xnT = self.xn_raw.bitcast(BF16).rearrange("p (c t) -> p c t", c=8)
        self.dxn = [[Dep() for _ in range(NTB)] for _ in range(8)]
        self.load_T(self.I["x"], 16, self.xT, self.dxT, 4)
        for layer in layers:
            if "mix" in parts:
                if layer == 0:
                    self.hybrid(layer, self.sub)
                else:
                    self.poolmix(layer)
            if "xa" in parts:
                self.xattn(layer)
            if "ffn" in parts:
                self.ffn(layer)
        self.final()
        self.P.barrier()
        self.P.emit("sp", lambda e: e.nop())


    def s5(self, ybT, dyb):
        A, I, P = self.A, self.I, self.P
        A.mark()
        B2 = Arena(self.xn_raw, 8 * T // 2)
        xT, dxT, xnT, dxn = self.xT, self.dxT, self.xnT, self.dxn
        win = I["w_in_ab"][0]
        Q = 512
        uT = A.alloc([4, T], BF16)
        du = [[Dep() for _ in range(NTB)] for _ in range(4)]
        wt = [A.alloc([8, 128], BF16), A.alloc([8, 128], BF16)]
        dwt = [Dep(), Dep()]
        for c in range(4):
            b = c % 2
            self.ldw(wt[b], win[:, 2056 + c * 128:2056 + (c + 1) * 128].rearrange("(k p) f -> p k f", p=128), dwt[b])
            for tb in range(NTB):
                bid = (c * NTB + tb) % 2
                bk, dbk = self.bank[bid], self.dbank[bid]
                self.proj_tile(wt[b], dwt[b], xnT, dxn, 8, tb, bk, dbk)
                self.cp("act" if tb % 2 else "dve", uT[:, c, tb * 512:(tb + 1) * 512], bk, [dbk], [du[c][tb]])
        P.barrier()
        dp = Dep()
        DP = [dp]
        names = ["lre", "lim", "ldt", "dt", "mag", "ang", "yv", "r", "fr", "afr", "sn", "cs", "lbr", "lbi",
                 "den", "t1", "t2", "cre", "cim", "ncim", "ncre", "fo1", "fo2", "fo3"]
        t = {n: A.alloc([16]) for n in names}
        for gh in range(2):
            ps_ = slice(gh * 64, (gh + 1) * 64)
            self.dma(t["lre"][ps_, :], I["ssm_lambda_re"][0].rearrange("(j gh) p -> gh p j", gh=2)[gh], [], DP, slow=True)
            self.dma(t["lim"][ps_, :], I["ssm_lambda_im"][0].rearrange("(j gh) p -> gh p j", gh=2)[gh], [], DP, slow=True)
            self.dma(t["ldt"][ps_, :], I["ssm_log_dt"][0].rearrange("(j gh) -> gh j", gh=2)[gh].partition_broadcast(64),
                     [], DP, slow=True)
        self.act(t["dt"], t["ldt"], AF.Exp, DP, DP)
        self.tt("dve", t["t1"], t["lre"], t["dt"], ALU.mult, DP, DP)
        self.act(t["mag"], t["t1"], AF.Exp, DP, DP)
        self.tt("dve", t["ang"], t["lim"], t["dt"], ALU.mult, DP, DP)
        self.ts("dve", t["yv"], t["ang"], 1.0 / TWO_PI, None, ALU.mult, None, DP, DP)
        self.ts("dve", t["r"], t["yv"], MAGIC, MAGIC, ALU.add, ALU.subtract, DP, DP)
        self.tt("dve", t["fr"], t["yv"], t["r"], ALU.subtract, DP, DP)
        self.act(t["afr"], t["fr"], AF.Abs, DP, DP)
        for q_ in (1, 2, 3):
            self.ts("dve", t["fo%d" % q_], t["fr"], float(q_ * 512), None, ALU.mult, None, DP, DP)
        self.act(t["sn"], t["fr"], AF.Sin, DP, DP, scale=TWO_PI)
        self.act(t["cs"], t["afr"], AF.Sin, DP, DP, scale=-TWO_PI, bias=self.halfpi)
        self.tt("dve", t["lbr"], t["mag"], t["cs"], ALU.mult, DP, DP)
        self.tt("dve", t["lbi"], t["mag"], t["sn"], ALU.mult, DP, DP)
        self.tt("dve", t["den"], t["lre"], t["lre"], ALU.mult, DP, DP)
        self.tt("dve", t["t1"], t["lim"], t["lim"], ALU.mult, DP, DP)
        self.tt("dve", t["den"], t["den"], t["t1"], ALU.add, DP, DP)
        P.emit("dve", lambda e: e.reciprocal(out=t["den"], in_=t["den"]), DP, DP)
        self.ts("dve", t["lbr"], t["lbr"], -1.0, None, ALU.add, None, DP, DP)
        self.tt("dve", t["t1"], t["lbr"], t["lre"], ALU.mult, DP, DP)
        self.tt("dve", t["t2"], t["lbi"], t["lim"], ALU.mult, DP, DP)
        self.tt("dve", t["t1"], t["t1"], t["t2"], ALU.add, DP, DP)
        self.tt("dve", t["cre"], t["t1"], t["den"], ALU.mult, DP, DP)
        self.tt("dve", t["t1"], t["lbi"], t["lre"], ALU.mult, DP, DP)
        self.tt("dve", t["t2"], t["lbr"], t["lim"], ALU.mult, DP, DP)
        self.tt("dve", t["t1"], t["t1"], t["t2"], ALU.subtract, DP, DP)
        self.tt("dve", t["cim"], t["t1"], t["den"], ALU.mult, DP, DP)
        self.ts("dve", t["ncim"], t["cim"], -1.0, None, ALU.mult, None, DP, DP)
        self.ts("dve", t["ncre"], t["cre"], -1.0, None, ALU.mult, None, DP, DP)
        cn = [A.alloc([16, 16]), A.alloc([16, 16])]
        bn = [A.alloc([16, 16]), A.alloc([16, 16])]
        for gh in range(2):
            ps_ = slice(gh * 64, (gh + 1) * 64)
            for ri, nm in enumerate(("ssm_c_re", "ssm_c_im")):
                src = I[nm][0].rearrange("(j gh) h p -> gh j p h", gh=2)[gh]
                for j_ in range(16):
                    self.dma(cn[ri][ps_, j_, :], src[j_], [], DP, slow=True)
            for ri, nm in enumerate(("ssm_b_re", "ssm_b_im")):
                self.dma(bn[ri][ps_], I[nm][0].rearrange("(j gh) p h -> gh p j h", gh=2)[gh], [], DP, slow=True)
        ce = [A.alloc([16, 16]), A.alloc([16, 16])]
        tmpc = A.alloc([16, 16])

        def bc(a):
            return a.unsqueeze(2).to_broadcast([128, 16, 16])
        self.tt("dve", ce[0], cn[0], bc(t["cre"]), ALU.mult, DP, DP)
        self.tt("dve", tmpc, cn[1], bc(t["ncim"]), ALU.mult, DP, DP)
        self.tt("dve", ce[0], ce[0], tmpc, ALU.add, DP, DP)
        self.tt("dve", ce[1], cn[0], bc(t["ncim"]), ALU.mult, DP, DP)
        self.tt("dve", tmpc, cn[1], bc(t["ncre"]), ALU.mult, DP, DP)
        self.tt("dve", ce[1], ce[1], tmpc, ALU.add, DP, DP)
        CE = A.alloc([16, 2, 32], BF16)
        self.memset("pool", CE, 0.0, DP)
        BP = A.alloc([2, 16, 32])
        self.memset("pool", BP, 0.0, DP)
        for gh in range(2):
            ps_ = slice(gh * 64, (gh + 1) * 64)
            for ri in range(2):
                self.cp("dve", CE[ps_, :, ri, gh * 16:(gh + 1) * 16], ce[ri][ps_], DP, DP)
                self.cp("dve", BP[ps_, ri, :, gh * 16:(gh + 1) * 16], bn[ri][ps_], DP, DP)
        BT = A.alloc([4, 2, 128], BF16)
        for c in range(4):
            for ri in range(2):
                bk, dbk = self.bank[2 + ri], self.dbank[2 + ri]
                self.tr(bk[:, 0:128], BP[:, ri, 4 * c:4 * c + 4, :].rearrange("p a b -> p (a b)"), self.ident,
                        DP + [self.dconst], [dbk])
                self.cp("dve", BT[:, c, ri, :], bk[:, 0:128], [dbk], DP)
        dvec = A.alloc([4])
        bglu = A.alloc([4])
        self.dma(dvec, I["ssm_d"][0].rearrange("g h -> (g h)").rearrange("(c p) -> p c", p=128), [], DP, slow=True)
        self.dma(bglu, I["b_glu_b"][0].rearrange("(c p) -> p c", p=128), [], DP, slow=True)
        iot = A.alloc([Q])
        P.emit("pool", lambda e: e.iota(iot, pattern=[[1, Q]], base=0, channel_multiplier=0,
                                        allow_small_or_imprecise_dtypes=True), [], DP)
        onesq = A.alloc([Q])
        self.memset("pool", onesq, 1.0, DP)
        def sc():
            return B2.alloc([Q]), Dep()
        yv, dyv = sc()
        fq, dfq = sc()
        TS, dTS = sc()
        TC, dTC = sc()
        bu = [sc(), sc()]
        pa = [sc() for _ in range(4)]
        magt, dmagt = sc()
        W = [sc(), sc()]
        carry = B2.alloc([2])
        dcarry = Dep()
        XR = B2.alloc([Q], BF16)
        XI = B2.alloc([Q], BF16)
        dX = [Dep(), Dep()]
        yp, dyp = B2.alloc([Q]), Dep()
        ygT = A.alloc([4, T], BF16)
        dyg = [[Dep() for _ in range(NTB)] for _ in range(4)]
        g1, dg1 = A.alloc([Q]), Dep()
        g2, dg2 = A.alloc([Q]), Dep()
        for c in range(4):
            for s_ in range(4):
                j = 4 * c + s_
                rows = slice(32 * s_, 32 * s_ + 32)
                self.ts("dve", magt, onesq, t["mag"][:, j:j + 1], None, ALU.mult, None, DP, [dmagt])
                for tb in range(NTB):
                    sl = slice(tb * Q, (tb + 1) * Q)
                    if tb == 0:
                        self.ts("dve", yv, iot, t["fr"][:, j:j + 1], None, ALU.mult, None, DP, [dyv])
                    else:
                        self.ts("dve", yv, iot, t["fr"][:, j:j + 1], t["fo%d" % tb][:, j:j + 1], ALU.mult, ALU.add,
                                DP, [dyv])
                    self.ts("dve", fq, yv, MAGIC, MAGIC, ALU.add, ALU.subtract, [dyv], [dfq])
                    self.tt("dve", fq, yv, fq, ALU.subtract, [dyv, dfq], [dfq])
                    self.act(yv, fq, AF.Abs, [dfq], [dyv])
                    self.act(TS, fq, AF.Sin, [dfq], [dTS], scale=TWO_PI)
                    self.act(TC, yv, AF.Sin, [dyv], [dTC], scale=-TWO_PI, bias=self.halfpi)
                    for ri in range(2):
                        bk, dbk = self.bank[ri], self.dbank[ri]
                        self.mm(bk, BT[rows, c, ri, :], uT[rows, c, sl], True, True, DP + [du[c][tb]], [dbk],
                                tile_position=(32 * s_, 0))
                        self.cp("act", bu[ri][0], bk, [dbk], [bu[ri][1]])
                    self.tt("dve", pa[0][0], TC, bu[0][0], ALU.mult, [dTC, bu[0][1]], [pa[0][1]])
                    self.tt("pool", pa[1][0], TS, bu[1][0], ALU.mult, [dTS, bu[1][1]], [pa[1][1]])
                    self.tt("dve", pa[2][0], TC, bu[1][0], ALU.mult, [dTC, bu[1][1]], [pa[2][1]])
                    self.tt("pool", pa[3][0], TS, bu[0][0], ALU.mult, [dTS, bu[0][1]], [pa[3][1]])
                    FR, dFR = pa[0]
                    FI, dFI = pa[2]
                    self.tt("dve", FR, pa[0][0], pa[1][0], ALU.add, [pa[0][1], pa[1][1]], [dFR])
                    self.tt("dve", FI, pa[2][0], pa[3][0], ALU.subtract, [pa[2][1], pa[3][1]], [dFI])
                    for ri, (F_, dF_) in enumerate(((FR, dFR), (FI, dFI))):
                        Wt, dW = W[ri]
                        if tb == 0:
                            P.emit("dve", lambda e, o=Wt, d0=magt, d1=F_: e.tensor_tensor_scan(
                                out=o, data0=d0, data1=d1, initial=0.0, op0=ALU.mult, op1=ALU.add),
                                [dmagt, dF_], [dW])
                        else:
                            P.emit("dve", lambda e, o=Wt, d0=magt, d1=F_, ini=carry[:, ri:ri + 1]: e.tensor_tensor_scan(
                                out=o, data0=d0, data1=d1, initial=ini, op0=ALU.mult, op1=ALU.add),
                                [dmagt, dF_, dcarry], [dW])
                    self.tt("dve", pa[0][0], TC, W[0][0], ALU.mult, [dTC, W[0][1]], [pa[0][1]])
                    self.tt("pool", pa[1][0], TS, W[1][0], ALU.mult, [dTS, W[1][1]], [pa[1][1]])
                    self.tt("dve", pa[2][0], TS, W[0][0], ALU.mult, [dTS, W[0][1]], [pa[2][1]])
                    self.tt("pool", pa[3][0], TC, W[1][0], ALU.mult, [dTC, W[1][1]], [pa[3][1]])
                    self.tt("dve", XR, pa[0][0], pa[1][0], ALU.subtract, [pa[0][1], pa[1][1]], [dX[0]])
                    self.tt("dve", XI, pa[2][0], pa[3][0], ALU.add, [pa[2][1], pa[3][1]], [dX[1]])
                    for ri in range(2):
                        self.cp("pool", carry[:, ri:ri + 1], W[ri][0][:, Q - 1:Q], [W[ri][1]], [dcarry])
                    bk, dbk = self.bank[4 + tb], self.dbank[4 + tb]
                    self.mm(bk[rows, :], CE[:, j, 0, :], XR, True, False, DP + [dX[0]], [dbk], tile_position=(0, 32 * s_))
                    self.mm(bk[rows, :], CE[:, j, 1, :], XI, False, True, DP + [dX[1]], [dbk], tile_position=(0, 32 * s_))
            for tb in range(NTB):
                sl = slice(tb * Q, (tb + 1) * Q)
                bk, dbk = self.bank[4 + tb], self.dbank[4 + tb]
                self.stt("dve", yp, uT[:, c, sl], dvec[:, c:c + 1], bk, ALU.mult, ALU.add, DP + [du[c][tb], dbk], [dyp])
                self.tt("dve", g1, yp, yp, ALU.mult, [dyp], [dg1])
                self.ts("dve", g1, g1, 0.044715, 1.0, ALU.mult, ALU.add, [dg1], [dg1])
                self.tt("dve", g1, g1, yp, ALU.mult, [dg1, dyp], [dg1])
                self.act(g2, g1, AF.Sigmoid, [dg1], [dg2], scale=2.0 * math.sqrt(2.0 / math.pi))
                self.tt("dve", ygT[:, c, sl], yp, g2, ALU.mult, [dyp, dg2], [dyg[c][tb]])
        wg = [A.alloc([4, 128], BF16), A.alloc([4, 128], BF16)]
        dwg = [Dep(), Dep()]
        for f in range(4):
            b = f % 2
            self.ldw(wg[b], I["w_glu_b"][0][:, f * 128:(f + 1) * 128].rearrange("(k p) f -> p k f", p=128), dwg[b])
            for tb in range(NTB):
                sl = slice(tb * Q, (tb + 1) * Q)
                bid = (f * NTB + tb) % 4
                bk, dbk = self.bank[bid], self.dbank[bid]
                self.proj_tile(wg[b], dwg[b], ygT, dyg, 4, tb, bk, dbk)
                self.act(g2, bk, AF.Sigmoid, [dbk] + DP, [dg2], bias=bglu[:, f:f + 1])
                self.tt("dve", ybT[:, f, sl], ygT[:, f, sl], g2, ALU.mult, [dyg[f][tb], dg2], [dyb[f][tb]])
        P.barrier()
        A.release()


    def deltanet(self, yaT, dya):
        A, I, P = self.A, self.I, self.P
        A.mark()
        xT, dxT, xnT, dxn = self.xT, self.dxT, self.xnT, self.dxn
        win = I["w_in_ab"][0]
        dsm = Dep()
        DS = [dsm]
        C = 128
        NCH = T // C
        stop = int(os.environ.get("DN_STOP", "9"))
        if stop < 9:
            for c_ in range(4):
                self.memset("pool", yaT[:, c_, :], 0.0, dya[c_])

        def small(shape=(C,)):
            return A.alloc(list(shape))
        wba = A.alloc([8, 8], BF16)
        dwba = Dep()
        self.ldw(wba, win[:, 2048:2056].rearrange("(k p) f -> p k f", p=128), dwba)
        bk, dbk = self.bank[0], self.dbank[0]
        for n in range(NCH):
            for k in range(8):
                self.mm(bk[:, n * 8:(n + 1) * 8], xnT[:, k, n * C:(n + 1) * C], wba[:, k, :], k == 0, k == 7,
                        [dwba, dxn[k][n // 4]], [dbk])
        BA = small((NCH, 8))
        self.cp("dve", BA, bk[:, 0:NCH * 8].rearrange("p (n e) -> p n e", e=8), [dbk], DS)
        alog = small((4,))
        dtb = small((4,))
        ong = small((1,))
        self.dma(alog, I["a_log_a"][0].partition_broadcast(128), [], DS, slow=True)
        self.dma(dtb, I["dt_bias_a"][0].partition_broadcast(128), [], DS, slow=True)
        self.dma(ong, I["onorm_g_a"][0].rearrange("(p o) -> p o", o=1), [], DS, slow=True)
        beta = small((NCH, 4))
        g_t = small((NCH, 4))
        gc_t = small((NCH, 4))
        gl_t = small((NCH, 4))
        bexp = small((NCH, 4))
        kdsc = small((NCH, 4))
        egl = small((NCH, 4))
        tm = small((NCH, 4))

        def bc4(a):
            return a.unsqueeze(1).to_broadcast([128, NCH, 4])
        self.act(beta, BA[:, :, 0:4], AF.Sigmoid, DS, DS)
        self.tt("dve", tm, BA[:, :, 4:8], bc4(dtb), ALU.add, DS, DS)
        self.act(tm, tm, AF.Exp, DS, DS)
        self.act(tm, tm, AF.Ln, DS, DS, bias=1.0)
        self.act(alog, alog, AF.Exp, DS, DS)
        self.tt("dve", g_t, tm, bc4(alog), ALU.mult, DS, DS)
        self.ts("dve", g_t, g_t, -1.0, None, ALU.mult, None, DS, DS)
        g2d = g_t.rearrange("p n h -> p (n h)")
        bk, dbk = self.bank[1], self.dbank[1]
        self.mm(bk[:, 0:64], self.tri, g2d, True, True, DS + [self.dconst], [dbk])
        self.cp("dve", gc_t.rearrange("p n h -> p (n h)"), bk[:, 0:64], [dbk], DS)
        self.mm(bk[:, 64:128], self.ones_f, g2d, True, True, DS + [self.dconst], [dbk])
        self.cp("dve", gl_t.rearrange("p n h -> p (n h)"), bk[:, 64:128], [dbk], DS)
        self.act(tm, gc_t, AF.Exp, DS, DS)
        self.tt("dve", bexp, beta, tm, ALU.mult, DS, DS)
        self.tt("dve", tm, gl_t, gc_t, ALU.subtract, DS, DS)
        self.act(kdsc, tm, AF.Exp, DS, DS)
        self.act(egl, gl_t, AF.Exp, DS, DS)
        cwq = small((4, 12))
        for i in range(4):
            self.dma(cwq[:, i, :], I["conv_qkv_a"][0, i].rearrange("(c p) -> p c", p=128), [], DS, slow=True)
        qkv = [A.alloc([T]) for _ in range(3)]
        dqkv = [Dep() for _ in range(3)]
        hb = A.alloc([3 + T])
        dhb = Dep()
        self.memset("pool", hb[:, 0:3], 0.0, [dhb])
        gsT = A.alloc([T], BF16)
        dgs = Dep()
        wt0 = A.alloc([8, 128], BF16)
        dwt0 = Dep()
        wt = [wt0, wt0]
        dwt = [dwt0, dwt0]
        rt, drt = A.alloc([512]), Dep()
        nm = ["trg", "e1", "dtm", "dnm", "egr", "qkt", "qgt", "rhsU", "rhsW", "kd", "u", "wT", "vnew", "on", "S",
              "junk", "Qa", "Qb", "Rf"]
        tls = [{n_: (small(), Dep()) for n_ in nm if n_ not in ("S", "junk", "on", "Rf", "vnew")} for _ in range(2)]
        tl = {"S": (small(), Dep())}
        vshared = (small(), Dep())
        for t_ in tls:
            t_["vnew"] = vshared
        PRs = [[(A.alloc([2, C]), Dep()), (A.alloc([2, C]), Dep())] for _ in range(2)]
        widx = 0
        nheads = int(os.environ.get("DN_HEADS", "4"))
        for c_ in range(nheads, 4):
            self.memset("pool", yaT[:, c_, :], 0.0, dya[c_])
        for h in range(nheads if stop >= 2 else 0):
            for qi in range(3):
                col = qi * 512 + h * 128
                ch = qi * 4 + h
                b = widx % 2
                widx += 1
                self.ldw(wt[b], win[:, col:col + 128].rearrange("(k p) f -> p k f", p=128), dwt[b])
                for tb in range(NTB):
                    bk, dbk = self.bank[4 + tb], self.dbank[4 + tb]
                    self.proj_tile(wt[b], dwt[b], xnT, dxn, 8, tb, bk, dbk)
                    self.cp("act" if tb % 2 else "dve", hb[:, 3 + tb * 512:3 + (tb + 1) * 512], bk, [dbk], [dhb])
                dst, dd = qkv[qi], dqkv[qi]
                self.ts("dve", dst, hb[:, 0:T], cwq[:, 0, ch:ch + 1], None, ALU.mult, None, [dhb] + DS, [dd])
                for tap in (1, 2, 3):
                    self.stt("dve", dst, hb[:, tap:tap + T], cwq[:, tap, ch:ch + 1], dst, ALU.mult, ALU.add,
                             [dhb, dd] + DS, [dd])
                self.act(dst, dst, AF.Silu, [dd], [dd])
                if qi < 2:
                    self.act(hb[:, 3:3 + T], dst, AF.Square, [dd], [dhb])
                    for tb in range(NTB):
                        sl = slice(tb * 512, (tb + 1) * 512)
                        bk, dbk = self.bank[tb % 2], self.dbank[tb % 2]
                        self.mm(bk, self.ones_f, hb[:, 3 + tb * 512:3 + (tb + 1) * 512], True, True,
                                [dhb, self.dconst], [dbk])
                        self.act(rt, bk, AF.Ln, [dbk], [drt], bias=1e-6)
                        self.act(rt, rt, AF.Exp, [drt], [drt], scale=-0.5)
                        if qi == 0:
                            self.stt("dve", dst[:, sl], dst[:, sl], 128.0 ** -0.5, rt, ALU.mult, ALU.mult,
                                     [dd, drt], [dd])
                        else:
                            self.tt("dve", dst[:, sl], dst[:, sl], rt, ALU.mult, [dd, drt], [dd])
            b = widx % 2
            widx += 1
            col = 1536 + h * 128
            self.ldw(wt[b], win[:, col:col + 128].rearrange("(k p) f -> p k f", p=128), dwt[b])
            for tb in range(NTB):
                bk, dbk = self.bank[4 + tb], self.dbank[4 + tb]
                self.proj_tile(wt[b], dwt[b], xnT, dxn, 8, tb, bk, dbk)
                self.act(gsT[:, tb * 512:(tb + 1) * 512], bk, AF.Silu, [dbk], [dgs])
            qT_, kT_, vT_ = qkv
            dq_, dk_, dv_ = dqkv
            S, dS_ = tl["S"]
            self.memset("pool", S, 0.0, [dS_])
            oh = hb[:, 3:3 + T]

            def mk(n):
                p = n % 2
                c_ = {"n": n, "cs": slice(n * C, (n + 1) * C), "t": tls[p]}
                c_["bA"], c_["dA"] = self.bank[p * 4 + 0], self.dbank[p * 4 + 0]
                c_["bB"] = [self.bank[p * 4 + 1], self.bank[p * 4 + 2]]
                c_["dB"] = [self.dbank[p * 4 + 1], self.dbank[p * 4 + 2]]
                c_["bC"], c_["dC"] = self.bank[p * 4 + 3], self.dbank[p * 4 + 3]
                c_["PR"] = list(PRs[p])
                c_["Q"] = [tls[p]["Qa"], tls[p]["Qb"]]
                return c_

            def dn_A(c_):
                n, cs, tl = c_["n"], c_["cs"], c_["t"]
                kc, qc = kT_[:, cs], qT_[:, cs]
                col = lambda a_: a_[:, n, h:h + 1]
                bA, dA = c_["bA"], c_["dA"]
                trg, dtrg = tl["trg"]
                self.ts("dve", trg, self.tri, col(g_t), None, ALU.mult, None, DS + [self.dconst], [dtrg])
                self.mm(bA[:, 0:C], self.ones_f, trg, True, True, [dtrg, self.dconst], [dA])
                e1, de1 = tl["e1"]
                e2, de2 = tl["trg"]
                dtm, ddtm = tl["dtm"]
                dnm, ddnm = tl["dnm"]
                egr, degr = tl["egr"]
                self.ts("dve", e1, bA[:, 0:C], col(gc_t), self.zcol[:, 0:1], ALU.subtract, ALU.min,
                        [dA, self.dconst] + DS, [de1])
                self.act(dtm, e1, AF.Exp, [de1], [ddtm])
                self.tt("dve", dtm, dtm, self.tri, ALU.mult, [ddtm, self.dconst], [ddtm])
                self.ts("dve", e2, bA[:, 0:C], col(gc_t), self.zcol[:, 0:1], ALU.subtract, ALU.max,
                        [dA, self.dconst] + DS, [de2])
                self.act(dnm, e2, AF.Exp, [de2], [ddnm], scale=-1.0)
                self.tt("dve", dnm, dnm, self.mlneg, ALU.mult, [ddnm, self.dconst], [ddnm])
                self.act(egr, bA[:, 0:C], AF.Exp, [dA], [degr])
                self.mm(bA[:, C:2 * C], kc, kc, True, True, [dk_], [dA])
                self.mm(bA[:, 2 * C:3 * C], kc, qc, True, True, [dk_, dq_], [dA])
                Qc, dQc = c_["Q"][0]
                self.stt("dve", Qc, bA[:, C:2 * C], col(beta), dnm, ALU.mult, ALU.mult, [dA, ddnm] + DS, [dQc])
                qkt, dqkt = tl["qkt"]
                self.tt("dve", qkt, bA[:, 2 * C:3 * C], dtm, ALU.mult, [dA, ddtm], [dqkt])
                qgt, dqgt = tl["qgt"]
                self.tt("dve", qgt, qc, egr, ALU.mult, [dq_, degr], [dqgt])
                self.tr(bA[:, 3 * C:4 * C], Qc, self.ident, [dQc, self.dconst], [dA])
                PRc, dPRc = c_["PR"][0]
                self.cp("act", PRc[:, 0, :], bA[:, 3 * C:4 * C], [dA], [dPRc])
                self.cp("dve", PRc[:, 1, :], self.ident, [self.dconst], [dPRc])

            def dn_N(c_, jj):
                (PRc, dPRc), (PRn, dPRn) = c_["PR"]
                (Qc, dQc), (Qn, dQn) = c_["Q"]
                bb, db = c_["bB"][jj % 2], c_["dB"][jj % 2]
                self.mm(bb[:, 0:2 * C], Qc, PRc.rearrange("p a b -> p (a b)"), True, True, [dQc, dPRc], [db])
                self.mm(bb[:, 2 * C:3 * C], PRc[:, 0, :], Qc, True, True, [dQc, dPRc], [db])
                self.cp("act", PRn[:, 0, :], bb[:, 0:C], [db], [dPRn])
                self.tt("dve", PRn[:, 1, :], PRc[:, 1, :], bb[:, C:2 * C], ALU.add, [dPRc, db], [dPRn])
                self.cp("act", Qn, bb[:, 2 * C:3 * C], [db], [dQn])
                c_["PR"] = [(PRn, dPRn), (PRc, dPRc)]
                c_["Q"] = [(Qn, dQn), (Qc, dQc)]

            def dn_F(c_):
                n, cs, tl = c_["n"], c_["cs"], c_["t"]
                kc, vc = kT_[:, cs], vT_[:, cs]
                col = lambda a_: a_[:, n, h:h + 1]
                (PRc, dPRc) = c_["PR"][0]
                (Qc, dQc) = c_["Q"][0]
                bb, db = c_["bB"][0], c_["dB"][0]
                bC, dC = c_["bC"], c_["dC"]
                Rf, dRf = tl["e1"]
                self.mm(bb[:, 0:C], Qc, PRc[:, 1, :], True, True, [dQc, dPRc], [db])
                self.tt("dve", Rf, PRc[:, 1, :], bb[:, 0:C], ALU.add, [dPRc, db], [dRf])
                rhsU, drU = tl["rhsU"]
                rhsW, drW = tl["rhsW"]
                kd, dkd = tl["kd"]
                self.tr(bC[:, 0:C], kc, self.ident, [dk_, self.dconst], [dC])
                self.tr(bC[:, C:2 * C], vc, self.ident, [dv_, self.dconst], [dC])
                self.ts("dve", rhsW, bC[:, 0:C], col(bexp), None, ALU.mult, None, [dC] + DS, [drW])
                self.ts("dve", kd, bC[:, 0:C], col(kdsc), None, ALU.mult, None, [dC] + DS, [dkd])
                self.ts("dve", rhsU, bC[:, C:2 * C], col(beta), None, ALU.mult, None, [dC] + DS, [drU])
                u_, du_ = tl["u"]
                wT_, dwT_ = tl["wT"]
                self.mm(bC[:, 2 * C:3 * C], Rf, rhsU, True, True, [dRf, drU], [dC])
                self.mm(bC[:, 3 * C:4 * C], rhsW, Rf, True, True, [dRf, drW], [dC])
                self.cp("act", u_, bC[:, 2 * C:3 * C], [dC], [du_])
                self.cp("act", wT_, bC[:, 3 * C:4 * C], [dC], [dwT_])

            def dn_S(c_):
                n, cs, tl = c_["n"], c_["cs"], c_["t"]
                col = lambda a_: a_[:, n, h:h + 1]
                u_, du_ = tl["u"]
                wT_, dwT_ = tl["wT"]
                qkt, dqkt = tl["qkt"]
                qgt, dqgt = tl["qgt"]
                kd, dkd = tl["kd"]
                vn, dvn = tl["vnew"]
                bb, db = c_["bB"][1], c_["dB"][1]
                self.mm(bb[:, 0:C], wT_, S, True, True, [dwT_, dS_], [db])
                self.tt("dve", vn, u_, bb[:, 0:C], ALU.subtract, [du_, db], [dvn])
                self.mm(bb[:, C:2 * C], S, qgt, True, True, [dqgt, dS_], [db])
                self.mm(bb[:, 3 * C:4 * C], vn, qkt, True, True, [dqkt, dvn], [db])
                self.mm(bb[:, 2 * C:3 * C], kd, vn, True, True, [dkd, dvn], [db])
                self.stt("dve", S, S, col(egl), bb[:, 2 * C:3 * C], ALU.mult, ALU.add, [dS_, db] + DS, [dS_])
                self.cp("dve", oh[:, cs], bb[:, C:2 * C], [db], [dhb])
                self.tt("dve", oh[:, cs], oh[:, cs], bb[:, 3 * C:4 * C], ALU.add, [dhb, db], [dhb])

            for n0 in range(0, NCH if stop >= 3 else 0, 2):
                cx = [mk(n0), mk(n0 + 1)]
                for c_ in cx:
                    dn_A(c_)
                for jj in range(6):
                    for c_ in cx:
                        dn_N(c_, jj)
                for c_ in cx:
                    dn_F(c_)
                for c_ in cx:
                    dn_S(c_)
            if stop < 7:
                continue
            oh = hb[:, 3:3 + T]
            sqb, dsqb = qT_, dq_
            self.act(sqb, oh, AF.Square, [dhb], [dsqb])
            for tb in range(NTB):
                sl = slice(tb * 512, (tb + 1) * 512)
                bk, dbk = self.bank[tb % 2], self.dbank[tb % 2]
                self.mm(bk, self.ones_f, sqb[:, sl], True, True, [dsqb, self.dconst], [dbk])
                self.act(rt, bk, AF.Ln, [dbk], [drt], scale=1.0 / 128.0, bias=EPS)
                self.act(rt, rt, AF.Exp, [drt], [drt], scale=-0.5)
                self.ts("dve", oh[:, sl], oh[:, sl], ong[:, 0:1], None, ALU.mult, None, [dhb] + DS, [dhb])
                self.tt("dve", oh[:, sl], oh[:, sl], rt, ALU.mult, [dhb, drt], [dhb])
                self.tt("pool", yaT[:, h, sl], oh[:, sl], gsT[:, sl], ALU.mult, [dhb, dgs], [dya[h][tb]])
        P.barrier()
        A.release()

    def hybrid(self, layer, sub=("s5", "dn")):
        A, I, P = self.A, self.I, self.P
        A.mark()
        xT, dxT, xnT, dxn = self.xT, self.dxT, self.xnT, self.dxn
        ybT = A.alloc([4, T], BF16)
        dyb = [[Dep() for _ in range(NTB)] for _ in range(4)]
        yaT = A.alloc([4, T], BF16)
        dya = [[Dep() for _ in range(NTB)] for _ in range(4)]
        self.rmsnorm(xT, dxT, 0, xnT, dxn, NTB, 512)
        if "s5" in sub:
            self.s5(ybT, dyb)
            self.rmsnorm(xT, dxT, 0, xnT, dxn, NTB, 512)
        else:
            for c in range(4):
                self.memset("pool", ybT[:, c, :], 0.0, dyb[c])
        if "dn" in sub:
            self.deltanet(yaT, dya)
        else:
            for c in range(4):
                self.memset("pool", yaT[:, c, :], 0.0, dya[c])
        wo = I["w_out_ab"][0]
        wt = [A.alloc([8, 128], BF16), A.alloc([8, 128], BF16)]
        dwt = [Dep(), Dep()]
        for f in range(8):
            b = f % 2
            self.ldw(wt[b], wo[:, f * 128:(f + 1) * 128].rearrange("(k p) f -> p k f", p=128), dwt[b])
            for tb in range(NTB):
                bid = (f * NTB + tb) % 2
                bk, dbk = self.bank[bid], self.dbank[bid]
                sl = slice(tb * 512, (tb + 1) * 512)
                for k in range(8):
                    src, dsrc = (yaT, dya) if k < 4 else (ybT, dyb)
                    self.mm(bk, wt[b][:, k, :], src[:, k % 4, sl], k == 0, k == 7, [dwt[b], dsrc[k % 4][tb]], [dbk])
                self.tt("dve", xT[:, f, sl], bk, xT[:, f, sl], ALU.add, [dbk, dxT[f][tb]], [dxT[f][tb]])
        P.barrier()
        A.release()


def build_nc(layers=(0, 1), parts=("mix", "xa", "ffn"), dbg=None, sub=("s5", "dn")):
    nc = bass.Bass("TRN2", target_bir_lowering=False)
    st = ExitStack()
    kb = KB(nc, st, dbg)
    kb.build(layers, parts, sub)
    kb.P.run(st)
    st.close()
    return nc, kb


def kernel(**inputs):
    nc, _ = build_nc()
    names = [n for n, _ in INPUT_SHAPES]
    shared = {n: np.ascontiguousarray(np.asarray(inputs[n], dtype=np.float32)) for n in names if n not in ("x", "mem")}
    x = np.asarray(inputs["x"], dtype=np.float32)
    mem = np.asarray(inputs["mem"], dtype=np.float32)
    in_maps = []
    for b in range(8):
        m = dict(shared)
        m["x"] = np.ascontiguousarray(x[b])
        m["mem"] = np.ascontiguousarray(mem[b])
        in_maps.append(m)
    res = run_bass_kernel_spmd(nc, in_maps, core_ids=list(range(8)))
    return np.stack([np.asarray(r["out"], dtype=np.float32) for r in res.results], axis=0)
```
